# Optimizing a Trainium2 kernel written in Bass

```python
import jax, jax.numpy as jnp
from jax import lax
import numpy as np

D_MODEL = 1024
BATCH = 8
SEQ = 8192
DEPTH = 2

HEAD_DIM = 64
D_MIX = D_MODEL
N_HEADS_TOTAL = D_MIX // HEAD_DIM
N_HEADS_DIL = (N_HEADS_TOTAL * 3) // 8
N_HEADS_FOX = (N_HEADS_TOTAL - N_HEADS_DIL) // 2
N_HEADS_SB = N_HEADS_TOTAL - N_HEADS_DIL - N_HEADS_FOX
DILATED_BRANCHES = ((128, 1), (512, 4), (2048, 16))
BLOCK = 128
D_FF = 4 * D_MODEL
N_MOD = 6
W_IN_COLS = 3 * D_MIX + N_HEADS_FOX
RMS_EPS = 1e-6
ATTN_SCALE = HEAD_DIM ** -0.5
ALIBI_MAX_BIAS = 8.0

kernel_name = "hybrid_dilated_fox_stickbreak_block"


def rmsnorm(x, g):
    xf = x.astype(jnp.float32)
    y = xf * lax.rsqrt(jnp.mean(xf * xf, axis=-1, keepdims=True) + RMS_EPS)
    return (y * g.astype(jnp.float32)).astype(x.dtype)


def alibi_slopes(n):
    return 2.0 ** (-ALIBI_MAX_BIAS * jnp.arange(1, n + 1, dtype=jnp.float32) / n)


def dilated_branch(q, k, v, window, dilation, slopes):
    B, S, H, Dh = q.shape
    span = window // dilation
    blk = span
    seg = blk * dilation
    Sp = -(-S // seg) * seg
    nb = Sp // seg

    def prep(a):
        a = jnp.pad(a, ((0, 0), (0, Sp - S), (0, 0), (0, 0)))
        return a.reshape(B, nb, blk, dilation, H, Dh)

    qb, kb, vb = prep(q), prep(k), prep(v)

    def with_prev(a):
        prev = jnp.pad(a, ((0, 0), (1, 0), (0, 0), (0, 0), (0, 0), (0, 0)))[:, :-1]
        return jnp.concatenate([prev, a], axis=2)

    kk, vv = with_prev(kb), with_prev(vb)
    scores = jnp.einsum('bnqrhd,bnkrhd->bnrhqk', qb, kk).astype(jnp.float32) * ATTN_SCALE

    qi = jnp.arange(blk)
    ki = jnp.arange(2 * blk)
    j = qi[:, None] + blk - ki[None, :]
    valid_qk = (j >= 0) & (j <= span)
    key_u = jnp.arange(nb)[:, None] * blk + ki[None, :] - blk
    mask = valid_qk[None] & (key_u >= 0)[:, None, :]
    alibi = -slopes[:, None, None] * (j * dilation).astype(jnp.float32)[None]

    logits = jnp.where(mask[None, :, None, None], scores + alibi[None, None, None], -jnp.inf)
    m = jnp.max(logits, axis=-1, keepdims=True)
    p = jnp.exp(logits - m)
    s = jnp.sum(p, axis=-1, keepdims=True)
    out = jnp.einsum('bnrhqk,bnkrhd->bnqrhd', p, vv.astype(jnp.float32))
    out = out / s.transpose(0, 1, 4, 2, 3, 5)
    lse = (m + jnp.log(s))[..., 0].transpose(0, 1, 4, 2, 3)
    out = out.reshape(B, Sp, H, Dh)[:, :S]
    lse = lse.reshape(B, Sp, H)[:, :S]
    return out, lse


def dilated_attention(q, k, v):
    slopes = alibi_slopes(q.shape[2])
    outs, lses = [], []
    for window, dilation in DILATED_BRANCHES:
        o, l = dilated_branch(q, k, v, window, dilation, slopes)
        outs.append(o)
        lses.append(l)
    w = jax.nn.softmax(jnp.stack(lses, axis=0), axis=0)
    out = jnp.sum(w[..., None] * jnp.stack(outs, axis=0), axis=0)
    return out.astype(q.dtype)


def forgetting_attention(q, k, v, f_logit):
    B, S, H, Dh = q.shape
    nb = S // BLOCK
    cum = jnp.cumsum(jax.nn.log_sigmoid(f_logit.astype(jnp.float32)), axis=1)
    cum_t = cum.transpose(0, 2, 1)
    qb = q.reshape(B, nb, BLOCK, H, Dh).transpose(1, 0, 2, 3, 4)
    cb = cum.reshape(B, nb, BLOCK, H).transpose(1, 0, 3, 2)
    kpos = jnp.arange(S)

    def one_block(args):
        n, qi, ci = args
        s = jnp.einsum('bqhd,bkhd->bhqk', qi, k).astype(jnp.float32) * ATTN_SCALE
        bias = ci[..., :, None] - cum_t[:, :, None, :]
        qpos = n * BLOCK + jnp.arange(BLOCK)
        mask = kpos[None, :] <= qpos[:, None]
        p = jax.nn.softmax(jnp.where(mask, s + bias, -jnp.inf), axis=-1)
        return jnp.einsum('bhqk,bkhd->bqhd', p.astype(v.dtype), v)

    out = lax.map(one_block, (jnp.arange(nb), qb, cb))
    return out.transpose(1, 0, 2, 3, 4).reshape(B, S, H, Dh)


def stick_breaking_attention(q, k, v):
    B, S, H, Dh = q.shape
    nb = S // BLOCK
    qb = q.reshape(B, nb, BLOCK, H, Dh).transpose(1, 0, 2, 3, 4)
    kpos = jnp.arange(S)

    def one_block(args):
        n, qi = args
        z = jnp.einsum('bqhd,bkhd->bhqk', qi, k).astype(jnp.float32) * ATTN_SCALE
        qpos = n * BLOCK + jnp.arange(BLOCK)
        mask = kpos[None, :] < qpos[:, None]
        log_beta = jax.nn.log_sigmoid(z)
        log_one_minus = jnp.where(mask, jax.nn.log_sigmoid(-z), 0.0)
        suffix = lax.cumsum(log_one_minus, axis=3, reverse=True) - log_one_minus
        a = jnp.where(mask, jnp.exp(log_beta + suffix), 0.0)
        return jnp.einsum('bhqk,bkhd->bqhd', a.astype(v.dtype), v)

    out = lax.map(one_block, (jnp.arange(nb), qb))
    return out.transpose(1, 0, 2, 3, 4).reshape(B, S, H, Dh)


def setup_inputs(seed: int = 0) -> dict:
    key = jax.random.key(seed)
    ks = jax.random.split(key, 16)
    f32 = jnp.float32
    x = jax.random.normal(ks[0], (BATCH, SEQ, D_MODEL), f32)
    c = jax.random.normal(ks[1], (BATCH, D_MODEL), f32)
    w_mod = jax.random.normal(ks[2], (DEPTH, D_MODEL, N_MOD * D_MODEL), f32) * (0.5 * D_MODEL ** -0.5)
    b_mod = jax.random.normal(ks[3], (DEPTH, N_MOD * D_MODEL), f32) * 0.02
    g_norm1 = 1.0 + 0.05 * jax.random.normal(ks[4], (DEPTH, D_MODEL), f32)
    w_in = jax.random.normal(ks[5], (DEPTH, D_MODEL, W_IN_COLS), f32) * D_MODEL ** -0.5
    b_f = jax.random.uniform(ks[6], (DEPTH, N_HEADS_FOX), f32, minval=1.0, maxval=4.0)
    g_out = 1.0 + 0.05 * jax.random.normal(ks[7], (DEPTH, D_MIX), f32)
    w_out = jax.random.normal(ks[8], (DEPTH, D_MIX, D_MODEL), f32) * D_MIX ** -0.5
    g_norm2 = 1.0 + 0.05 * jax.random.normal(ks[9], (DEPTH, D_MODEL), f32)
    w_mlp_in = jax.random.normal(ks[10], (DEPTH, D_MODEL, D_FF), f32) * D_MODEL ** -0.5
    w_mlp_out = jax.random.normal(ks[11], (DEPTH, D_FF, D_MODEL), f32) * D_FF ** -0.5
    g_final = 1.0 + 0.05 * jax.random.normal(ks[12], (D_MODEL,), f32)
    return {"x": x, "c": c, "w_mod": w_mod, "b_mod": b_mod, "g_norm1": g_norm1, "w_in": w_in,
            "b_f": b_f, "g_out": g_out, "w_out": w_out, "g_norm2": g_norm2,
            "w_mlp_in": w_mlp_in, "w_mlp_out": w_mlp_out, "g_final": g_final}


def reference(x, c, w_mod, b_mod, g_norm1, w_in, b_f, g_out, w_out, g_norm2, w_mlp_in, w_mlp_out, g_final):
    B, S, D = x.shape
    h0 = N_HEADS_DIL
    h1 = N_HEADS_DIL + N_HEADS_FOX
    c_act = jax.nn.silu(c)
    for l in range(DEPTH):
        mod = c_act @ w_mod[l] + b_mod[l]
        sh1, sc1, g1, sh2, sc2, g2 = jnp.split(mod, N_MOD, axis=-1)

        h = rmsnorm(x, g_norm1[l]) * (1.0 + sc1[:, None]) + sh1[:, None]
        proj = h @ w_in[l]
        q = proj[..., :D_MIX].reshape(B, S, N_HEADS_TOTAL, HEAD_DIM)
        k = proj[..., D_MIX:2 * D_MIX].reshape(B, S, N_HEADS_TOTAL, HEAD_DIM)
        v = proj[..., 2 * D_MIX:3 * D_MIX].reshape(B, S, N_HEADS_TOTAL, HEAD_DIM)
        f_logit = proj[..., 3 * D_MIX:] + b_f[l]

        o_dil = dilated_attention(q[:, :, :h0], k[:, :, :h0], v[:, :, :h0])
        o_fox = forgetting_attention(q[:, :, h0:h1], k[:, :, h0:h1], v[:, :, h0:h1], f_logit)
        o_sb = stick_breaking_attention(q[:, :, h1:], k[:, :, h1:], v[:, :, h1:])
        o = jnp.concatenate([o_dil, o_fox.astype(x.dtype), o_sb.astype(x.dtype)], axis=2)
        o = rmsnorm(o, g_out[l].reshape(N_HEADS_TOTAL, HEAD_DIM)).reshape(B, S, D_MIX)
        x = x + g1[:, None] * (o @ w_out[l])

        h = rmsnorm(x, g_norm2[l]) * (1.0 + sc2[:, None]) + sh2[:, None]
        hid = jax.nn.relu(h @ w_mlp_in[l])
        x = x + g2[:, None] * ((hid * hid) @ w_mlp_out[l])
    return rmsnorm(x, g_final)
```

```python
import contextlib
import numpy as np
import concourse.bass as bass
import concourse.mybir as mybir
from concourse.bass_utils import run_bass_kernel_spmd

F32 = mybir.dt.float32
BF16 = mybir.dt.bfloat16
AF = mybir.ActivationFunctionType
ALU = mybir.AluOpType

D = 1024
KC = 8
T = 512
NH = 16
HD = 64
N_DIL, N_FOX, N_SB = 6, 5, 5
DFF = 4096
W_IN_COLS = 3 * D + N_FOX
EPS = 1e-6
NEG = -1.0e9
N_CORES = 8


class Buf:
    __slots__ = ("name", "last_write", "readers", "ds")

    def __init__(self, name):
        self.name = name
        self.last_write = None
        self.readers = []
        self.ds = {}


class Sched:
    def __init__(self, nc, stack):
        self.nc = nc
        self.stack = stack
        self.E = {"pe": nc.tensor, "act": nc.scalar, "dve": nc.vector, "pool": nc.gpsimd, "sp": nc.sync}
        self.sem = {}
        self.cnt = {}
        for e in ("pe", "act", "dve", "pool"):
            self.sem[e] = stack.enter_context(nc.semaphore("s_" + e))
            self.cnt[e] = 0
        self.seen = {e: {} for e in self.E}
        self.ninst = 0
        self.dma_bufs = []
        self.free_sems = {"hw": [], "sw": []}
        self.nsem = 0

    def _wait(self, e, rec):
        _, sem, val = rec
        d = self.seen[e]
        k = id(sem)
        if d.get(k, 0) >= val:
            return
        d[k] = val
        self.E[e].wait_ge(sem, val)
        self.ninst += 1

    @staticmethod
    def _compact(readers):
        best = {}
        for r in readers:
            k = id(r[1])
            if k not in best or best[k][2] < r[2]:
                best[k] = r
        return list(best.values())

    def _finish(self, rec, reads, writes):
        for b in reads:
            b.readers.append(rec)
            if len(b.readers) > 6:
                b.readers = self._compact(b.readers)
        for b in writes:
            b.last_write = rec
            b.readers = []

    def _wait_all(self, e, recs):
        best = {}
        for r in recs:
            k = id(r[1])
            if k not in best or best[k][2] < r[2]:
                best[k] = r
        for r in best.values():
            self._wait(e, r)

    def op(self, e, fn, reads=(), writes=()):
        recs = []
        for b in reads:
            if b.last_write is not None:
                recs.append(b.last_write)
        for b in writes:
            w = b.last_write
            if w is not None and w[0] != e:
                recs.append(w)
            for r in b.readers:
                if r[0] == e:
                    continue
                recs.append(r)
        self._wait_all(e, recs)
        ins = fn(self.E[e])
        self.cnt[e] += 1
        ins.then_inc(self.sem[e], 1)
        self.ninst += 1
        self._finish((e, self.sem[e], self.cnt[e]), reads, writes)
        return ins

    def dma(self, q, out, in_, reads=(), writes=(), owner=None, **kw):
        recs = []
        for b in reads:
            if b.last_write is not None:
                recs.append(b.last_write)
        for b in writes:
            if b.last_write is not None:
                recs.append(b.last_write)
            recs.extend(b.readers)
        self._wait_all(q, recs)
        if owner is None:
            owner = (list(writes) + list(reads))[0]
        qc = "sw" if q == "pool" else "hw"
        d = owner.ds.get(qc)
        if d is None:
            if self.free_sems[qc]:
                d = self.free_sems[qc].pop()
            else:
                d = [self.stack.enter_context(self.nc.semaphore("d%s%d" % (qc, self.nsem))), 0]
                self.nsem += 1
            owner.ds[qc] = d
            if owner not in self.dma_bufs:
                self.dma_bufs.append(owner)
        d[1] += 16
        ins = self.E[q].dma_start(out=out, in_=in_, **kw)
        ins.then_inc(d[0], 16)
        self.ninst += 1
        self._finish(("dma", d[0], d[1]), reads, writes)
        return ins

    def barrier(self):
        for e in self.E:
            for e2 in ("pe", "act", "dve", "pool"):
                if self.cnt[e2] > 0 and e2 != e:
                    self._wait(e, (e2, self.sem[e2], self.cnt[e2]))
            for b in self.dma_bufs:
                for d in b.ds.values():
                    if d[1] > 0:
                        self._wait(e, ("dma", d[0], d[1]))

    def release(self, keep=()):
        kept = []
        for b in self.dma_bufs:
            if b in keep:
                kept.append(b)
            else:
                for qc, d in b.ds.items():
                    self.free_sems[qc].append(d)
                b.ds = {}
        self.dma_bufs = kept


def _const_tables():
    j = np.arange(128)[:, None]
    slopes = 2.0 ** (-8.0 * np.arange(1, N_DIL + 1, dtype=np.float64) / N_DIL)
    d = np.arange(23)[None, :, None] - 3
    i = np.arange(128)[None, None, :]
    delta = 128 * d + i - j[:, :, None]
    cnt = ((delta >= 0) & (delta <= 128)).astype(np.float64)
    cnt += ((delta >= 0) & (delta <= 512) & (delta % 4 == 0))
    cnt += ((delta >= 0) & (delta <= 2048) & (delta % 16 == 0))
    mt = np.zeros((N_DIL, 128, 23, 128), np.float64)
    for h in range(N_DIL):
        mt[h] = cnt * np.exp(-slopes[h] * np.maximum(delta, 0))
    mt = mt.reshape(N_DIL, 128, 23 * 128).astype(np.float32)
    u = np.arange(896)[None, :]
    neg_incl = np.where(j > u - 384, NEG, 0.0).astype(np.float32)
    neg_strict = np.where(j >= u - 384, NEG, 0.0).astype(np.float32)
    s = np.arange(128)[None, :]
    ident = (j == s).astype(np.float32)
    ntri_incl = np.where(j >= s, -1.0, 0.0).astype(np.float32)
    ntri_strc = np.where(j < s, -1.0, 0.0).astype(np.float32)
    ones = np.ones((128, 128), np.float32)
    wss = np.zeros((128, 64), np.float32)
    wss[0:64] = 1.0 / 64.0
    wss[64] = EPS
    wss2 = wss.copy()
    wss2[64] = 0.0
    cb = np.concatenate([neg_incl, neg_strict, ident, ntri_incl, ntri_strc, ones, np.pad(wss, ((0, 0), (0, 64))),
                         np.pad(wss2, ((0, 0), (0, 64)))], axis=1)
    return mt, np.ascontiguousarray(cb)


CB_NI, CB_NS, CB_ID, CB_TI, CB_TS, CB_ON, CB_WS = 0, 896, 1792, 1920, 2048, 2176, 2304
CB_WS2 = 2432
CB_W = 2560


def build(S, L):
    NT = S // T
    NB = S // 128
    nc = bass.Bass("TRN2", target_bir_lowering=False)

    def dram(name, shape, dt, kind):
        return nc.dram_tensor(name, shape, dt, kind=kind).ap()

    xT_in = dram("xT", [D, S], F32, "ExternalInput")
    c_col_d = dram("c_col", [128, KC], F32, "ExternalInput")
    w_mod_d = dram("w_mod", [L, D, 6 * D], F32, "ExternalInput")
    bmod_d = dram("bmod_col", [L, 128, 48], F32, "ExternalInput")
    g1_d = dram("g1_col", [L, 128, KC], F32, "ExternalInput")
    g2_d = dram("g2_col", [L, 128, KC], F32, "ExternalInput")
    gout_d = dram("gout_col", [L, 64, NH], F32, "ExternalInput")
    gf_d = dram("gf_col", [128, KC], F32, "ExternalInput")
    bf_d = dram("bf_col", [L, N_FOX, 1], F32, "ExternalInput")
    w_in_d = dram("w_in", [L, D, W_IN_COLS], F32, "ExternalInput")
    w_out_d = dram("w_out", [L, D, D], F32, "ExternalInput")
    w1_d = dram("w1", [L, D, DFF], F32, "ExternalInput")
    w2_d = dram("w2", [L, DFF, D], F32, "ExternalInput")
    mt_d = dram("mtab", [N_DIL, 128, 23 * 128], F32, "ExternalInput")
    cb_d = dram("cb", [128, CB_W], F32, "ExternalInput")
    outT = dram("outT", [D, S], F32, "ExternalOutput")

    xT_s = dram("xT_s", [D, S], F32, "Internal")
    qT_s = dram("qT_s", [KC, 128, S], BF16, "Internal")
    kT_s = dram("kT_s", [KC, 128, S], BF16, "Internal")
    v_s = dram("v_s", [S, D], BF16, "Internal")
    oT_s = dram("oT_s", [KC, 128, S], BF16, "Internal")
    h2T_s = dram("h2T_s", [KC, 128, S], BF16, "Internal")
    fl_s = dram("fl_s", [N_FOX, S], F32, "Internal")
    faq_s = dram("faq_s", [N_FOX, 6, S], BF16, "Internal")
    fak_s = dram("fak_s", [N_FOX, 6, S], BF16, "Internal")

    with contextlib.ExitStack() as gs:
        sch = Sched(nc, gs)

        uniq = [0]

        def sb(stack, name, shape, dt):
            uniq[0] += 1
            name = "%s_%d" % (name, uniq[0])
            t = stack.enter_context(nc.sbuf_tensor(name, shape, dt))
            return t, Buf(name)

        def sbn(stack, name, shape, dt, n):
            return [sb(stack, "%s%d" % (name, i), shape, dt) for i in range(n)]

        class BankView:
            def __init__(self, t, off):
                self.t, self.off = t, off

            def __getitem__(self, idx):
                r, c = idx
                c0 = (c.start or 0) + self.off
                c1 = (512 if c.stop is None else c.stop) + self.off
                return self.t[r, c0:c1]

        psd = [gs.enter_context(nc.psum_tensor("psd%d" % i, [128, 1024], F32)) for i in range(4)]
        ps = [(BankView(psd[i // 2], (i % 2) * 512), Buf("ps%d" % i)) for i in range(8)]

        cbt, cbb = sb(gs, "cbt", [128, CB_W], BF16)
        sch.dma("pool", out=cbt[:, :], in_=cb_d[:, :], writes=[cbb])
        negI = cbt[:, CB_NI:CB_NI + 896]
        negS = cbt[:, CB_NS:CB_NS + 896]
        ident = cbt[:, CB_ID:CB_ID + 128]
        ntri_i = cbt[:, CB_TI:CB_TI + 128]
        ntri_s = cbt[:, CB_TS:CB_TS + 128]
        ones_bf = cbt[:, CB_ON:CB_ON + 128]

        prm, prmb = sb(gs, "prm", [128, 256], F32)
        def pc(l, off, n=1):
            return prm[:, 96 * l + off: 96 * l + off + n]

        sch.dma("sp", out=prm[:, 192:200], in_=gf_d[:, :], writes=[prmb])
        sch.dma("sp", out=prm[:, 200:208], in_=c_col_d[:, :], writes=[prmb])
        for l in range(L):
            sch.dma("sp", out=pc(l, 64, 8), in_=g1_d[l], writes=[prmb])
            sch.dma("sp", out=pc(l, 72, 8), in_=g2_d[l], writes=[prmb])
            sch.dma("sp", out=prm[0:64, 96 * l + 80: 96 * l + 96], in_=gout_d[l], writes=[prmb])
            sch.dma("sp", out=prm[0:N_FOX, 216 + l: 217 + l], in_=bf_d[l], writes=[prmb])
            sch.dma("sp", out=pc(l, 0, 48), in_=bmod_d[l], writes=[prmb])

        with contextlib.ExitStack() as st:
            wm = sbn(st, "wm", [128, KC, 512], F32, 2)
            sch.op("act", lambda e: e.activation(out=prm[:, 224:232], in_=prm[:, 200:208], func=AF.Exp, scale=-1.0),
                   reads=[prmb], writes=[prmb])
            sch.op("dve", lambda e: e.tensor_scalar(out=prm[:, 224:232], in0=prm[:, 224:232], scalar1=1.0, scalar2=None, op0=ALU.add),
                   reads=[prmb], writes=[prmb])
            sch.op("dve", lambda e: e.reciprocal(out=prm[:, 232:240], in_=prm[:, 224:232]), reads=[prmb], writes=[prmb])
            sch.op("dve", lambda e: e.tensor_tensor(out=prm[:, 208:216], in0=prm[:, 200:208], in1=prm[:, 232:240], op=ALU.mult),
                   reads=[prmb], writes=[prmb])
            for l in range(L):
                sch.op("dve", lambda e: e.tensor_scalar(out=prm[0:N_FOX, 216 + l:217 + l], in0=prm[0:N_FOX, 216 + l:217 + l],
                                                        scalar1=-1.0, scalar2=None, op0=ALU.mult), reads=[prmb], writes=[prmb])
            mps, mpsb = ps[7]
            blk = 0
            for l in range(L):
                for cbk in range(12):
                    wt, wb = wm[blk % 2]
                    blk += 1
                    sch.dma("sp", out=wt[:, :, :],
                            in_=w_mod_d[l].rearrange("(k p) c -> p k c", p=128)[:, :, cbk * 512:(cbk + 1) * 512], writes=[wb])
                    for jj in range(4):
                        j = cbk * 4 + jj
                        for k in range(KC):
                            sch.op("pe", lambda e: e.matmul(mps[:, l * 48 + j: l * 48 + j + 1], lhsT=wt[:, k, jj * 128:(jj + 1) * 128],
                                                            rhs=prm[:, 208 + k: 209 + k], start=(k == 0), stop=(k == KC - 1)),
                                   reads=[wb, prmb], writes=[mpsb])
                sch.op("dve", lambda e: e.tensor_tensor(out=pc(l, 0, 48), in0=mps[:, l * 48:(l + 1) * 48], in1=pc(l, 0, 48), op=ALU.add),
                       reads=[mpsb, prmb], writes=[prmb])
                sch.op("dve", lambda e: e.scalar_tensor_tensor(out=pc(l, 48, 8), in0=pc(l, 8, 8), scalar=1.0, in1=pc(l, 64, 8),
                                                               op0=ALU.add, op1=ALU.mult), reads=[prmb], writes=[prmb])
                sch.op("dve", lambda e: e.scalar_tensor_tensor(out=pc(l, 56, 8), in0=pc(l, 32, 8), scalar=1.0, in1=pc(l, 72, 8),
                                                               op0=ALU.add, op1=ALU.mult), reads=[prmb], writes=[prmb])
            sch.barrier()
            sch.release(keep=(cbb, prmb))

        def rms_rstd(xt, xtb, sq, sqb, ssp, sspb, lnv, lnvb, rstd, rstdb):
            sch.op("act", lambda e: e.activation(out=sq[:, :, :], in_=xt[:, :, :], func=AF.Square), reads=[xtb], writes=[sqb])
            for k in range(KC):
                sch.op("pe", lambda e: e.matmul(ssp[:, :], lhsT=ones_bf, rhs=sq[:, k, :], start=(k == 0), stop=(k == KC - 1)),
                       reads=[sqb, cbb], writes=[sspb])
            sch.op("act", lambda e: e.activation(out=lnv[:, :], in_=ssp[:, :], func=AF.Ln, scale=1.0 / D, bias=EPS),
                   reads=[sspb], writes=[lnvb])
            sch.op("act", lambda e: e.activation(out=rstd[:, :], in_=lnv[:, :], func=AF.Exp, scale=-0.5), reads=[lnvb], writes=[rstdb])

        sch.op("dve", lambda e: e.memset(prm[:, 240:241], EPS), writes=[prmb])
        sch.op("dve", lambda e: e.memset(prm[:, 241:242], 1.0), writes=[prmb])
        sch.op("dve", lambda e: e.memset(prm[:, 242:243], 0.0), writes=[prmb])
        eps_col = prm[:, 240:241]
        one_col = prm[:, 241:242]
        zero_col = prm[:, 242:243]

        def xview(ap_ds):
            return ap_ds.rearrange("(k p) s -> p k s", p=128)

        def load_w_cast(dst, src_rows_ap, name, ncols, nk=KC, bw=512):
            bufs = []
            c0 = 0
            while c0 < ncols:
                c1 = min(ncols, c0 + bw)
                b = Buf("%s_c%d" % (name, c0))
                sch.dma("pool", out=dst[:, :, c0:c1], in_=src_rows_ap.rearrange("(k p) c -> p k c", p=128)[:, :, c0:c1], writes=[b])
                bufs.append(b)
                c0 = c1
            return bufs

        for l in range(L):
            xsrc = xT_in if l == 0 else xT_s
            with contextlib.ExitStack() as st:
                win, winb = sb(st, "win", [128, KC, W_IN_COLS], BF16)
                xts = sbn(st, "xt", [128, KC, T], F32, 2)
                sq, sqb = sb(st, "sq", [128, KC, T], BF16)
                tmps = sbn(st, "tmp", [128, T], F32, 2)
                lnv, lnvb = sb(st, "lnv", [128, T], F32)
                rstd, rstdb = sb(st, "rstd", [128, T], F32)
                hTs = sbn(st, "hT", [128, KC, T], BF16, 2)
                qk, qkb = sb(st, "qk", [128, 16, T], BF16)
                vsb, vsbb = sb(st, "vsb", [128, 4, D], BF16)
                fe, feb = sb(st, "fe", [N_FOX, T], F32)
                fsp, fspb = sb(st, "fsp", [N_FOX, T], F32)
                winbs = load_w_cast(win, w_in_d[l], "win", W_IN_COLS)

                def loadx(t):
                    xt, xtb = xts[t % 2]
                    sch.dma("sp", out=xt[:, :, :], in_=xview(xsrc)[:, :, t * T:(t + 1) * T], writes=[xtb])

                loadx(0)
                nev = 0
                for t in range(NT):
                    if t + 1 < NT:
                        loadx(t + 1)
                    xt, xtb = xts[t % 2]
                    hT, hTb = hTs[t % 2]
                    ssp, sspb = ps[4 + t % 2]
                    rms_rstd(xt, xtb, sq, sqb, ssp, sspb, lnv, lnvb, rstd, rstdb)
                    for k in range(KC):
                        tm, tmb = tmps[k % 2]
                        sch.op("dve", lambda e: e.tensor_tensor(out=tm[:, :], in0=xt[:, k, :], in1=rstd[:, :], op=ALU.mult),
                               reads=[xtb, rstdb], writes=[tmb])
                        sch.op("act", lambda e: e.activation(out=hT[:, k, :], in_=tm[:, :], func=AF.Identity,
                                                             bias=pc(l, 0 + k), scale=pc(l, 48 + k)),
                               reads=[tmb, prmb], writes=[hTb])
                    for c in range(16):
                        pp, ppb = ps[c % 4]
                        for k in range(KC):
                            sch.op("pe", lambda e: e.matmul(pp[:, :], lhsT=win[:, k, c * 128:(c + 1) * 128], rhs=hT[:, k, :],
                                                            start=(k == 0), stop=(k == KC - 1)), reads=[winbs[c // 4], hTb], writes=[ppb])
                        if nev % 2 == 0:
                            sch.op("act", lambda e: e.copy(out=qk[:, c, :], in_=pp[:, :]), reads=[ppb], writes=[qkb])
                        else:
                            sch.op("dve", lambda e: e.tensor_copy(out=qk[:, c, :], in_=pp[:, :]), reads=[ppb], writes=[qkb])
                        nev += 1
                    sch.dma("pool", out=qT_s.rearrange("k p s -> p k s")[:, :, t * T:(t + 1) * T], in_=qk[:, 0:8, :], reads=[qkb])
                    sch.dma("pool", out=kT_s.rearrange("k p s -> p k s")[:, :, t * T:(t + 1) * T], in_=qk[:, 8:16, :], reads=[qkb])
                    for tb in range(4):
                        for half in range(2):
                            pp, ppb = ps[nev % 4]
                            for k in range(KC):
                                sch.op("pe", lambda e: e.matmul(pp[:, :], lhsT=hT[:, k, tb * 128:(tb + 1) * 128],
                                                                rhs=win[:, k, 2 * D + half * 512: 2 * D + (half + 1) * 512],
                                                                start=(k == 0), stop=(k == KC - 1)), reads=[winbs[4 + half], hTb], writes=[ppb])
                            if nev % 2 == 0:
                                sch.op("act", lambda e: e.copy(out=vsb[:, tb, half * 512:(half + 1) * 512], in_=pp[:, :]),
                                       reads=[ppb], writes=[vsbb])
                            else:
                                sch.op("dve", lambda e: e.tensor_copy(out=vsb[:, tb, half * 512:(half + 1) * 512], in_=pp[:, :]),
                                       reads=[ppb], writes=[vsbb])
                            nev += 1
                    sch.dma("pool", out=v_s.rearrange("(n p) c -> p n c", p=128)[:, t * 4:(t + 1) * 4, :], in_=vsb[:, :, :], reads=[vsbb])
                    fp, fpb = ps[6]
                    for k in range(KC):
                        sch.op("pe", lambda e: e.matmul(fp[0:N_FOX, :], lhsT=win[:, k, 3 * D:3 * D + N_FOX], rhs=hT[:, k, :],
                                                        start=(k == 0), stop=(k == KC - 1)), reads=[winbs[6], hTb], writes=[fpb])
                    sch.op("act", lambda e: e.activation(out=fe[:, :], in_=fp[0:N_FOX, :], func=AF.Exp, scale=-1.0,
                                                         bias=prm[0:N_FOX, 216 + l:217 + l]), reads=[fpb, prmb], writes=[feb])
                    sch.op("act", lambda e: e.activation(out=fsp[:, :], in_=fe[:, :], func=AF.Ln, bias=one_col[0:N_FOX, :]),
                           reads=[feb, prmb], writes=[fspb])
                    sch.dma("pool", out=fl_s[:, t * T:(t + 1) * T], in_=fsp[:, :], reads=[fspb])
                sch.barrier()
                sch.release(keep=(cbb, prmb))

            with contextlib.ExitStack() as st:
                CH = min(S, 2048)
                onesf, onesfb = sb(st, "onesf", [N_FOX, CH], F32)
                ones3, ones3b = sb(st, "ones3", [N_FOX, 3, CH], BF16)
                fls = sbn(st, "fl", [N_FOX, CH], F32, 2)
                G, Gb = sb(st, "G", [N_FOX, CH], F32)
                G8, G8b = sb(st, "G8", [N_FOX, CH], F32)
                r1, r1b = sb(st, "r1", [N_FOX, CH], F32)
                cpos = sbn(st, "cpos", [N_FOX, 3, CH], BF16, 2)
                cneg = sbn(st, "cneg", [N_FOX, 3, CH], BF16, 2)
                carry, carryb = sb(st, "carry", [N_FOX, 2], F32)
                sch.op("dve", lambda e: e.memset(onesf[:, :], 1.0), writes=[onesfb])
                sch.op("dve", lambda e: e.memset(ones3[:, :, :], 1.0), writes=[ones3b])
                sch.op("dve", lambda e: e.memset(carry[:, :], 0.0), writes=[carryb])
                for ci in range(S // CH):
                    fl, flb = fls[ci % 2]
                    cp, cpb = cpos[ci % 2]
                    cn, cnb = cneg[ci % 2]
                    sl = slice(ci * CH, (ci + 1) * CH)
                    sch.dma("sp", out=fl[:, :], in_=fl_s[:, sl], writes=[flb])
                    sch.op("dve", lambda e: e.tensor_tensor_scan(out=G[:, :], data0=onesf[:, :], data1=fl[:, :], initial=carry[:, 0:1],
                                                                 op0=ALU.mult, op1=ALU.add), reads=[onesfb, flb, carryb], writes=[Gb])
                    sch.op("dve", lambda e: e.tensor_copy(out=carry[:, 0:1], in_=G[:, CH - 1:CH]), reads=[Gb], writes=[carryb])
                    sch.op("dve", lambda e: e.tensor_scalar(out=G8[:, :], in0=G[:, :], scalar1=8.0, scalar2=None, op0=ALU.mult),
                           reads=[Gb], writes=[G8b])
                    sch.op("dve", lambda e: e.tensor_copy(out=cp[:, 0, :], in_=G8[:, :]), reads=[G8b], writes=[cpb])
                    sch.op("dve", lambda e: e.tensor_tensor(out=r1[:, :], in0=G8[:, :], in1=cp[:, 0, :], op=ALU.subtract),
                           reads=[G8b, cpb], writes=[r1b])
                    sch.op("dve", lambda e: e.tensor_copy(out=cp[:, 1, :], in_=r1[:, :]), reads=[r1b], writes=[cpb])
                    sch.op("dve", lambda e: e.tensor_tensor(out=G8[:, :], in0=r1[:, :], in1=cp[:, 1, :], op=ALU.subtract),
                           reads=[r1b, cpb], writes=[G8b])
                    sch.op("dve", lambda e: e.tensor_copy(out=cp[:, 2, :], in_=G8[:, :]), reads=[G8b], writes=[cpb])
                    sch.op("dve", lambda e: e.tensor_scalar(out=cn[:, :, :], in0=cp[:, :, :], scalar1=-1.0, scalar2=None, op0=ALU.mult),
                           reads=[cpb], writes=[cnb])
                    sch.dma("pool", out=fak_s[:, 0:3, sl], in_=cp[:, :, :], reads=[cpb])
                    sch.dma("pool", out=fak_s[:, 3:6, sl], in_=ones3[:, :, :], reads=[ones3b])
                    sch.dma("pool", out=faq_s[:, 0:3, sl], in_=ones3[:, :, :], reads=[ones3b])
                    sch.dma("pool", out=faq_s[:, 3:6, sl], in_=cn[:, :, :], reads=[cnb])
                sch.barrier()
                sch.release(keep=(cbb, prmb))

            with contextlib.ExitStack() as st:
                qTs = sbn(st, "qT", [70, S], BF16, 2)
                kTs = sbn(st, "kT", [70, S], BF16, 2)
                vs = sbn(st, "v", [128, NB, 65], BF16, 2)
                mts = sbn(st, "mt", [128, 23 * 128], F32, 2)
                Es = sbn(st, "E", [128, 2 * T], F32, 3)
                Lps = sbn(st, "Lp", [128, 2 * T], BF16, 4)
                EGs = sbn(st, "EG", [128, T], F32, 2)
                pTs = sbn(st, "pT", [128, 2 * T], BF16, 3)
                sqs = sbn(st, "sqo", [65, T], BF16, 2)
                lns = sbn(st, "lno", [64, T], F32, 2)
                rrs = sbn(st, "rro", [64, T], F32, 2)
                osb = sbn(st, "osb", [64, T], BF16, 2)
                wss = cbt[:, CB_WS:CB_WS + 64]
                for s_ in range(2):
                    vt, vb = vs[s_]
                    sch.op("dve", lambda e: e.memset(vt[:, :, 64:65], 1.0), writes=[vb])
                    sch.op("dve", lambda e: e.memset(qTs[s_][0][64:70, :], 0.0), writes=[qTs[s_][1]])
                    sch.op("dve", lambda e: e.memset(kTs[s_][0][64:70, :], 0.0), writes=[kTs[s_][1]])

                def load_head(h):
                    s_ = h % 2
                    qt, qb = qTs[s_]
                    kt, kb = kTs[s_]
                    vt, vb = vs[s_]
                    r0 = (h % 2) * 64
                    sch.dma("sp", out=qt[0:64, :], in_=qT_s[h // 2, r0:r0 + 64, :], writes=[qb])
                    sch.dma("sp", out=kt[0:64, :], in_=kT_s[h // 2, r0:r0 + 64, :], writes=[kb])
                    if h >= N_DIL + N_FOX:
                        sch.op("dve", lambda e: e.memset(kt[64:70, :], 0.0), writes=[kb])
                    if N_DIL <= h < N_DIL + N_FOX:
                        hf = h - N_DIL
                        sch.dma("sp", out=qt[64:70, :], in_=faq_s[hf], writes=[qb])
                        sch.dma("sp", out=kt[64:70, :], in_=fak_s[hf], writes=[kb])
                    sch.dma("sp", out=vt[:, :, 0:64], in_=v_s.rearrange("(n p) c -> p n c", p=128)[:, :, h * 64:(h + 1) * 64],
                            writes=[vb])
                    if h < N_DIL:
                        mt, mb = mts[s_]
                        sch.dma("sp", out=mt[:, :], in_=mt_d[h], writes=[mb])

                epi_n = [0]

                def attn_head(h):
                    s_ = h % 2
                    kind = "dil" if h < N_DIL else ("fox" if h < N_DIL + N_FOX else "sb")
                    K = 70
                    qt, qb = qTs[s_]
                    kt, kb = kTs[s_]
                    vt, vb = vs[s_]
                    mt, mb = mts[s_]
                    pairs = []
                    for i in range(NT):
                        if kind == "dil":
                            ms = [m for m in range(4 * i - 16, 4 * i + 4) if m >= 0]
                        elif kind == "fox":
                            ms = list(range(0, 4 * i + 4))
                        else:
                            ms = list(range(4 * i + 3, -1, -1))
                        for idx, m in enumerate(ms):
                            pairs.append((i, m, idx == 0, idx == len(ms) - 1))
                    n = len(pairs)

                    nsp = n // 2
                    assert n % 2 == 0

                    def stage1(pi):
                        i, m, first, last = pairs[pi]
                        stt, stb = ps[((pi // 2) % 2) * 2 + (pi % 2)]
                        diag = (m >= 4 * i) and kind != "dil"
                        sch.op("pe", lambda e: e.matmul(stt[:, :], lhsT=kt[0:K, m * 128:(m + 1) * 128], rhs=qt[0:K, i * T:(i + 1) * T],
                                                        start=True, stop=not diag), reads=[kb, qb], writes=[stb])
                        if diag:
                            r = m - 4 * i
                            tab = negI if kind == "fox" else negS
                            sch.op("pe", lambda e: e.matmul(stt[:, :], lhsT=ident, rhs=tab[:, 384 - 128 * r: 896 - 128 * r],
                                                            start=False, stop=True), reads=[cbb], writes=[stb])

                    def epilogue(i):
                        ot, otb = ps[5 + i % 2]
                        mst, msb = ps[7]
                        en = epi_n[0]
                        epi_n[0] += 1
                        sqt, sqb_ = sqs[en % 2]
                        lnt, lnb = lns[en % 2]
                        rrt, rrb = rrs[en % 2]
                        ost, osb_ = osb[en % 2]
                        R = 65
                        wcol = CB_WS2 if kind == "sb" else CB_WS
                        bval = EPS if kind == "sb" else 0.0
                        sch.op("act", lambda e: e.activation(out=sqt[0:R, :], in_=ot[0:R, :], func=AF.Square), reads=[otb], writes=[sqb_])
                        sch.op("pe", lambda e: e.matmul(mst[:, :], lhsT=cbt[0:R, wcol:wcol + 128], rhs=sqt[0:R, :], start=True, stop=True),
                               reads=[sqb_, cbb], writes=[msb])
                        sch.op("act", lambda e: e.activation(out=lnt[:, :], in_=mst[0:64, :], func=AF.Ln, bias=bval),
                               reads=[msb], writes=[lnb])
                        sch.op("act", lambda e: e.activation(out=rrt[:, :], in_=lnt[:, :], func=AF.Exp, scale=-0.5), reads=[lnb], writes=[rrb])
                        sch.op("dve", lambda e: e.scalar_tensor_tensor(out=ost[:, :], in0=ot[0:64, :], scalar=prm[0:64, 96 * l + 80 + h: 96 * l + 81 + h],
                                                                       in1=rrt[:, :], op0=ALU.mult, op1=ALU.mult),
                               reads=[otb, rrb, prmb], writes=[osb_])
                        r0 = (h % 2) * 64
                        sch.dma("pool", out=oT_s[h // 2, r0:r0 + 64, i * T:(i + 1) * T], in_=ost[:, :], reads=[osb_])

                    def hv(t, pi):
                        return t[:, (pi % 2) * T:(pi % 2 + 1) * T]

                    def S1(k):
                        stage1(2 * k)
                        stage1(2 * k + 1)

                    def S2(k):
                        slot = k % 2
                        sup = psd[slot]
                        b0, b1 = ps[slot * 2][1], ps[slot * 2 + 1][1]
                        if kind == "fox":
                            pt, pb = pTs[k % 3]
                            sch.op("act", lambda e: e.activation(out=pt[:, :], in_=sup[:, :], func=AF.Exp, scale=0.125),
                                   reads=[b0, b1], writes=[pb])
                        else:
                            Et, Eb = Es[k % 3]
                            sch.op("act", lambda e: e.activation(out=Et[:, :], in_=sup[:, :], func=AF.Exp, scale=0.125),
                                   reads=[b0, b1], writes=[Eb])
                            if kind == "dil":
                                pt, pb = pTs[k % 3]
                                for pi in (2 * k, 2 * k + 1):
                                    i, m, first, last = pairs[pi]
                                    d0 = 4 * i - m
                                    sch.op("dve", lambda e: e.tensor_tensor(out=hv(pt, pi), in0=hv(Et, pi),
                                                                            in1=mt[:, (d0 + 3) * 128:(d0 + 7) * 128], op=ALU.mult),
                                           reads=[Eb, mb], writes=[pb])

                    def S2_lp(k):
                        Et, Eb = Es[k % 3]
                        Lt, Lb = Lps[k % 4]
                        sch.op("act", lambda e: e.activation(out=Lt[:, :], in_=Et[:, :], func=AF.Ln, bias=1.0), reads=[Eb], writes=[Lb])

                    def tri1(pi):
                        i, m, first, last = pairs[pi]
                        Lt, Lb = Lps[(pi // 2) % 4]
                        act_, accb = ps[4]
                        sch.op("pe", lambda e: e.matmul(act_[:, :], lhsT=ntri_i, rhs=hv(Lt, pi), start=first, stop=True, skip_group_check=not first),
                               reads=[Lb, cbb], writes=[accb])

                    def tri2(pi):
                        i, m, first, last = pairs[pi]
                        if last:
                            return
                        Lt, Lb = Lps[(pi // 2) % 4]
                        act_, accb = ps[4]
                        sch.op("pe", lambda e: e.matmul(act_[:, :], lhsT=ntri_s, rhs=hv(Lt, pi), start=False, stop=True, skip_group_check=True),
                               reads=[Lb, cbb], writes=[accb])

                    def eg(pi):
                        pt, pb = pTs[(pi // 2) % 3]
                        Et, Eb = Es[(pi // 2) % 3]
                        Gt, Gb_ = EGs[pi % 2]
                        act_, accb = ps[4]
                        sch.op("act", lambda e: e.activation(out=Gt[:, :], in_=act_[:, :], func=AF.Exp), reads=[accb], writes=[Gb_])
                        sch.op("dve", lambda e: e.tensor_tensor(out=hv(pt, pi), in0=hv(Et, pi), in1=Gt[:, :], op=ALU.mult),
                               reads=[Eb, Gb_], writes=[pb])

                    def pv(pi):
                        i, m, first, last = pairs[pi]
                        ot, otb = ps[5 + i % 2]
                        pt, pb = pTs[(pi // 2) % 3]
                        R = 65
                        sch.op("pe", lambda e: e.matmul(ot[0:R, :], lhsT=vt[:, m, 0:R], rhs=hv(pt, pi), start=first, stop=last),
                               reads=[vb, pb], writes=[otb])
                        if last:
                            epilogue(i)

                    if kind == "sb":
                        for k in range(nsp + 4):
                            a = 2 * (k - 3)
                            b = a + 1
                            pb_prev = a - 1
                            if 0 <= pb_prev < n:
                                tri2(pb_prev)
                            if 0 <= a < n:
                                tri1(a)
                            if k < nsp:
                                S1(k)
                            if 0 <= a < n:
                                eg(a)
                            if 0 <= k - 1 < nsp:
                                S2(k - 1)
                            if 0 <= a < n:
                                tri2(a)
                                tri1(b)
                            if 0 <= pb_prev < n:
                                pv(pb_prev)
                            if 0 <= b < n:
                                eg(b)
                            if 0 <= k - 1 < nsp:
                                S2_lp(k - 1)
                            if 0 <= a < n:
                                pv(a)
                    else:
                        for k in range(nsp + 2):
                            if k < nsp:
                                S1(k)
                            if 0 <= k - 1 < nsp:
                                S2(k - 1)
                            if 0 <= k - 2 < nsp:
                                pv(2 * (k - 2))
                                pv(2 * (k - 2) + 1)

                load_head(0)
                for h in range(NH):
                    if h + 1 < NH:
                        load_head(h + 1)
                    attn_head(h)
                sch.barrier()
                sch.release(keep=(cbb, prmb))

            with contextlib.ExitStack() as st:
                wo, wob = sb(st, "wo", [128, KC, D], BF16)
                xts = sbn(st, "xt", [128, KC, T], F32, 2)
                oTs = sbn(st, "oTt", [128, KC, T], BF16, 2)
                sq, sqb = sb(st, "sq", [128, KC, T], BF16)
                tmps = sbn(st, "tmp", [128, T], F32, 2)
                lnv, lnvb = sb(st, "lnv", [128, T], F32)
                rstd, rstdb = sb(st, "rstd", [128, T], F32)
                hTs = sbn(st, "hT", [128, KC, T], BF16, 2)
                wobs = load_w_cast(wo, w_out_d[l], "wo", D)

                def loadc1(t):
                    xt, xtb = xts[t % 2]
                    ott, otb = oTs[t % 2]
                    sch.dma("sp", out=xt[:, :, :], in_=xview(xsrc)[:, :, t * T:(t + 1) * T], writes=[xtb])
                    sch.dma("sp", out=ott[:, :, :], in_=oT_s.rearrange("k p s -> p k s")[:, :, t * T:(t + 1) * T], writes=[otb])

                loadc1(0)
                for t in range(NT):
                    if t + 1 < NT:
                        loadc1(t + 1)
                    xt, xtb = xts[t % 2]
                    ott, otb = oTs[t % 2]
                    hT, hTb = hTs[t % 2]
                    for c in range(KC):
                        pp, ppb = ps[c % 4]
                        for k in range(KC):
                            sch.op("pe", lambda e: e.matmul(pp[:, :], lhsT=wo[:, k, c * 128:(c + 1) * 128], rhs=ott[:, k, :],
                                                            start=(k == 0), stop=(k == KC - 1)), reads=[wobs[c // 4], otb], writes=[ppb])
                        sch.op("dve", lambda e: e.scalar_tensor_tensor(out=xt[:, c, :], in0=pp[:, :], scalar=pc(l, 16 + c), in1=xt[:, c, :],
                                                                       op0=ALU.mult, op1=ALU.add), reads=[ppb, xtb, prmb], writes=[xtb])
                    ssp, sspb = ps[4 + t % 2]
                    rms_rstd(xt, xtb, sq, sqb, ssp, sspb, lnv, lnvb, rstd, rstdb)
                    for k in range(KC):
                        tm, tmb = tmps[k % 2]
                        sch.op("dve", lambda e: e.tensor_tensor(out=tm[:, :], in0=xt[:, k, :], in1=rstd[:, :], op=ALU.mult),
                               reads=[xtb, rstdb], writes=[tmb])
                        sch.op("act", lambda e: e.activation(out=hT[:, k, :], in_=tm[:, :], func=AF.Identity,
                                                             bias=pc(l, 24 + k), scale=pc(l, 56 + k)),
                               reads=[tmb, prmb], writes=[hTb])
                    sch.dma("pool", out=xview(xT_s)[:, :, t * T:(t + 1) * T], in_=xt[:, :, :], reads=[xtb])
                    sch.dma("pool", out=h2T_s.rearrange("k p s -> p k s")[:, :, t * T:(t + 1) * T], in_=hT[:, :, :], reads=[hTb])
                sch.barrier()
                sch.release(keep=(cbb, prmb))

            for half in range(2):
                last_phase = (l == L - 1 and half == 1)
                with contextlib.ExitStack() as st:
                    HH = DFF // 2
                    w1t, w1b = sb(st, "w1t", [128, KC, HH], BF16)
                    w2t, w2b = sb(st, "w2t", [128, HH // 128, D], BF16)
                    xts = sbn(st, "xt", [128, KC, T], F32, 2)
                    hTs = sbn(st, "hT", [128, KC, T], BF16, 2)
                    rbs = sbn(st, "rb", [128, T], F32, 2)
                    hid, hidb = sb(st, "hid", [128, HH // 128, T], BF16)
                    if last_phase:
                        sq, sqb = sb(st, "sq", [128, KC, T], BF16)
                        lnv, lnvb = sb(st, "lnv", [128, T], F32)
                        rstd, rstdb = sb(st, "rstd", [128, T], F32)
                        outs = sbn(st, "ot", [128, KC, T], F32, 1)
                    w1bs = load_w_cast(w1t, w1_d[l][:, half * HH:(half + 1) * HH], "w1", HH)
                    w2bs = load_w_cast(w2t, w2_d[l][half * HH:(half + 1) * HH, :], "w2", D, nk=HH // 128, bw=256)

                    def loadc2(t):
                        xt, xtb = xts[t % 2]
                        hT, hTb = hTs[t % 2]
                        sch.dma("sp", out=xt[:, :, :], in_=xview(xT_s)[:, :, t * T:(t + 1) * T], writes=[xtb])
                        sch.dma("sp", out=hT[:, :, :], in_=h2T_s.rearrange("k p s -> p k s")[:, :, t * T:(t + 1) * T], writes=[hTb])

                    loadc2(0)
                    for t in range(NT):
                        if t + 1 < NT:
                            loadc2(t + 1)
                        xt, xtb = xts[t % 2]
                        hT, hTb = hTs[t % 2]
                        for j in range(HH // 128):
                            pp, ppb = ps[j % 4]
                            rb, rbb = rbs[j % 2]
                            for k in range(KC):
                                sch.op("pe", lambda e: e.matmul(pp[:, :], lhsT=w1t[:, k, j * 128:(j + 1) * 128], rhs=hT[:, k, :],
                                                                start=(k == 0), stop=(k == KC - 1)), reads=[w1bs[j // 4], hTb], writes=[ppb])
                            sch.op("act", lambda e: e.activation(out=rb[:, :], in_=pp[:, :], func=AF.Relu), reads=[ppb], writes=[rbb])
                            sch.op("dve", lambda e: e.scalar_tensor_tensor(out=hid[:, j, :], in0=pp[:, :], scalar=0.0, in1=rb[:, :],
                                                                           op0=ALU.max, op1=ALU.mult), reads=[ppb, rbb], writes=[hidb])
                        for c in range(KC):
                            pp, ppb = ps[4 + c % 4]
                            nj = HH // 128
                            for j in range(nj):
                                sch.op("pe", lambda e: e.matmul(pp[:, :], lhsT=w2t[:, j, c * 128:(c + 1) * 128], rhs=hid[:, j, :],
                                                                start=(j == 0), stop=(j == nj - 1)), reads=[w2bs[c // 2], hidb], writes=[ppb])
                            sch.op("dve", lambda e: e.scalar_tensor_tensor(out=xt[:, c, :], in0=pp[:, :], scalar=pc(l, 40 + c), in1=xt[:, c, :],
                                                                           op0=ALU.mult, op1=ALU.add), reads=[ppb, xtb, prmb], writes=[xtb])
                        if not last_phase:
                            sch.dma("pool", out=xview(xT_s)[:, :, t * T:(t + 1) * T], in_=xt[:, :, :], reads=[xtb])
                        else:
                            ot, otb = outs[0]
                            ssp, sspb = ps[0]
                            rms_rstd(xt, xtb, sq, sqb, ssp, sspb, lnv, lnvb, rstd, rstdb)
                            for k in range(KC):
                                sch.op("dve", lambda e: e.scalar_tensor_tensor(out=ot[:, k, :], in0=xt[:, k, :], scalar=prm[:, 192 + k:193 + k],
                                                                               in1=rstd[:, :], op0=ALU.mult, op1=ALU.mult),
                                       reads=[xtb, rstdb, prmb], writes=[otb])
                            sch.dma("pool", out=xview(outT)[:, :, t * T:(t + 1) * T], in_=ot[:, :, :], reads=[otb])
                    sch.barrier()
                    sch.release(keep=(cbb, prmb))
                sch.release(keep=(cbb, prmb))
        sch.barrier()
        print("instructions emitted:", sch.ninst)
    return nc


_CACHE = {}


def kernel(x, c, w_mod, b_mod, g_norm1, w_in, b_f, g_out, w_out, g_norm2, w_mlp_in, w_mlp_out, g_final):
    x = np.asarray(x, np.float32)
    B, S, _ = x.shape
    L = int(np.asarray(w_mod).shape[0])
    assert B == N_CORES
    key = (S, L)
    if key not in _CACHE:
        _CACHE[key] = build(S, L)
    nc = _CACHE[key]
    mt, cb = _const_tables()
    f = lambda a: np.ascontiguousarray(np.asarray(a, np.float32))
    col = lambda a: np.ascontiguousarray(np.asarray(a, np.float32).reshape(-1, 128).T)
    shared = {
        "w_mod": f(w_mod),
        "bmod_col": np.stack([col(np.asarray(b_mod)[l]) for l in range(L)]),
        "g1_col": np.stack([col(np.asarray(g_norm1)[l]) for l in range(L)]),
        "g2_col": np.stack([col(np.asarray(g_norm2)[l]) for l in range(L)]),
        "gout_col": np.stack([np.ascontiguousarray(np.asarray(g_out, np.float32)[l].reshape(NH, HD).T) for l in range(L)]),
        "gf_col": col(g_final),
        "bf_col": f(b_f).reshape(L, N_FOX, 1),
        "w_in": f(w_in), "w_out": f(w_out), "w1": f(w_mlp_in), "w2": f(w_mlp_out),
        "mtab": mt, "cb": cb,
    }
    c = np.asarray(c, np.float32)
    in_maps = []
    for b in range(B):
        m = dict(shared)
        m["xT"] = np.ascontiguousarray(x[b].T)
        m["c_col"] = col(c[b])
        in_maps.append(m)
    res = run_bass_kernel_spmd(nc, in_maps, core_ids=list(range(N_CORES)))
    out = np.stack([np.ascontiguousarray(res.results[b]["outT"].T) for b in range(B)], axis=0)
    return out.astype(np.float32)
```

```python
import contextlib
import numpy as np
import concourse.bass as bass
import concourse.mybir as mybir
from concourse.bass_utils import run_bass_kernel_spmd

F32 = mybir.dt.float32
BF16 = mybir.dt.bfloat16
AF = mybir.ActivationFunctionType
ALU = mybir.AluOpType

D = 1024
KC = 8
T = 512
NH = 16
HD = 64
N_DIL, N_FOX, N_SB = 6, 5, 5
DFF = 4096
W_IN_COLS = 3 * D + N_FOX
EPS = 1e-6
NEG = -1.0e9
N_CORES = 8


class Buf:
    __slots__ = ("name", "last_write", "readers", "ds")

    def __init__(self, name):
        self.name = name
        self.last_write = None
        self.readers = []
        self.ds = {}


class Sched:
    def __init__(self, nc, stack):
        self.nc = nc
        self.stack = stack
        self.E = {"pe": nc.tensor, "act": nc.scalar, "dve": nc.vector, "pool": nc.gpsimd, "sp": nc.sync}
        self.sem = {}
        self.cnt = {}
        for e in ("pe", "act", "dve", "pool"):
            self.sem[e] = stack.enter_context(nc.semaphore("s_" + e))
            self.cnt[e] = 0
        self.seen = {e: {} for e in self.E}
        self.ninst = 0
        self.dma_bufs = []
        self.free_sems = {"hw": [], "sw": []}
        self.nsem = 0

    def _wait(self, e, rec):
        _, sem, val = rec
        d = self.seen[e]
        k = id(sem)
        if d.get(k, 0) >= val:
            return
        d[k] = val
        self.E[e].wait_ge(sem, val)
        self.ninst += 1

    @staticmethod
    def _compact(readers):
        best = {}
        for r in readers:
            k = id(r[1])
            if k not in best or best[k][2] < r[2]:
                best[k] = r
        return list(best.values())

    def _finish(self, rec, reads, writes):
        for b in reads:
            b.readers.append(rec)
            if len(b.readers) > 6:
                b.readers = self._compact(b.readers)
        for b in writes:
            b.last_write = rec
            b.readers = []

    def _wait_all(self, e, recs):
        best = {}
        for r in recs:
            k = id(r[1])
            if k not in best or best[k][2] < r[2]:
                best[k] = r
        for r in best.values():
            self._wait(e, r)

    def op(self, e, fn, reads=(), writes=()):
        recs = []
        for b in reads:
            if b.last_write is not None:
                recs.append(b.last_write)
        for b in writes:
            w = b.last_write
            if w is not None and w[0] != e:
                recs.append(w)
            for r in b.readers:
                if r[0] == e:
                    continue
                recs.append(r)
        self._wait_all(e, recs)
        ins = fn(self.E[e])
        self.cnt[e] += 1
        ins.then_inc(self.sem[e], 1)
        self.ninst += 1
        self._finish((e, self.sem[e], self.cnt[e]), reads, writes)
        return ins

    def dma(self, q, out, in_, reads=(), writes=(), owner=None, **kw):
        recs = []
        for b in reads:
            if b.last_write is not None:
                recs.append(b.last_write)
        for b in writes:
            if b.last_write is not None:
                recs.append(b.last_write)
            recs.extend(b.readers)
        self._wait_all(q, recs)
        if owner is None:
            owner = (list(writes) + list(reads))[0]
        qc = "sw" if q == "pool" else "hw"
        d = owner.ds.get(qc)
        if d is None:
            if self.free_sems[qc]:
                d = self.free_sems[qc].pop()
            else:
                d = [self.stack.enter_context(self.nc.semaphore("d%s%d" % (qc, self.nsem))), 0]
                self.nsem += 1
            owner.ds[qc] = d
            if owner not in self.dma_bufs:
                self.dma_bufs.append(owner)
        d[1] += 16
        ins = self.E[q].dma_start(out=out, in_=in_, **kw)
        ins.then_inc(d[0], 16)
        self.ninst += 1
        self._finish(("dma", d[0], d[1]), reads, writes)
        return ins

    def barrier(self):
        for e in self.E:
            for e2 in ("pe", "act", "dve", "pool"):
                if self.cnt[e2] > 0 and e2 != e:
                    self._wait(e, (e2, self.sem[e2], self.cnt[e2]))
            for b in self.dma_bufs:
                for d in b.ds.values():
                    if d[1] > 0:
                        self._wait(e, ("dma", d[0], d[1]))

    def release(self, keep=()):
        kept = []
        for b in self.dma_bufs:
            if b in keep:
                kept.append(b)
            else:
                for qc, d in b.ds.items():
                    self.free_sems[qc].append(d)
                b.ds = {}
        self.dma_bufs = kept


def _const_tables():
    j = np.arange(128)[:, None]
    slopes = 2.0 ** (-8.0 * np.arange(1, N_DIL + 1, dtype=np.float64) / N_DIL)
    d = np.arange(23)[None, :, None] - 3
    i = np.arange(128)[None, None, :]
    delta = 128 * d + i - j[:, :, None]
    cnt = ((delta >= 0) & (delta <= 128)).astype(np.float64)
    cnt += ((delta >= 0) & (delta <= 512) & (delta % 4 == 0))
    cnt += ((delta >= 0) & (delta <= 2048) & (delta % 16 == 0))
    mt = np.zeros((N_DIL, 128, 23, 128), np.float64)
    for h in range(N_DIL):
        mt[h] = cnt * np.exp(-slopes[h] * np.maximum(delta, 0))
    mt = mt.reshape(N_DIL, 128, 23 * 128).astype(np.float32)
    u = np.arange(896)[None, :]
    neg_incl = np.where(j > u - 384, NEG, 0.0).astype(np.float32)
    neg_strict = np.where(j >= u - 384, NEG, 0.0).astype(np.float32)
    s = np.arange(128)[None, :]
    ident = (j == s).astype(np.float32)
    ntri_incl = np.where(j >= s, -1.0, 0.0).astype(np.float32)
    ntri_strc = np.where(j < s, -1.0, 0.0).astype(np.float32)
    ones = np.ones((128, 128), np.float32)
    wss = np.zeros((128, 64), np.float32)
    wss[0:64] = 1.0 / 64.0
    wss[64] = EPS
    wss2 = wss.copy()
    wss2[64] = 0.0
    cb = np.concatenate([neg_incl, neg_strict, ident, ntri_incl, ntri_strc, ones, np.pad(wss, ((0, 0), (0, 64))),
                         np.pad(wss2, ((0, 0), (0, 64)))], axis=1)
    return mt, np.ascontiguousarray(cb)


CB_NI, CB_NS, CB_ID, CB_TI, CB_TS, CB_ON, CB_WS = 0, 896, 1792, 1920, 2048, 2176, 2304
CB_WS2 = 2432
CB_W = 2560


def build(S, L):
    NT = S // T
    NB = S // 128
    nc = bass.Bass("TRN2", target_bir_lowering=False)

    def dram(name, shape, dt, kind):
        return nc.dram_tensor(name, shape, dt, kind=kind).ap()

    xT_in = dram("xT", [D, S], F32, "ExternalInput")
    c_col_d = dram("c_col", [128, KC], F32, "ExternalInput")
    w_mod_d = dram("w_mod", [L, D, 6 * D], F32, "ExternalInput")
    bmod_d = dram("bmod_col", [L, 128, 48], F32, "ExternalInput")
    g1_d = dram("g1_col", [L, 128, KC], F32, "ExternalInput")
    g2_d = dram("g2_col", [L, 128, KC], F32, "ExternalInput")
    gout_d = dram("gout_col", [L, 64, NH], F32, "ExternalInput")
    gf_d = dram("gf_col", [128, KC], F32, "ExternalInput")
    bf_d = dram("bf_col", [L, N_FOX, 1], F32, "ExternalInput")
    w_in_d = dram("w_in", [L, D, W_IN_COLS], F32, "ExternalInput")
    w_out_d = dram("w_out", [L, D, D], F32, "ExternalInput")
    w1_d = dram("w1", [L, D, DFF], F32, "ExternalInput")
    w2_d = dram("w2", [L, DFF, D], F32, "ExternalInput")
    mt_d = dram("mtab", [N_DIL, 128, 23 * 128], F32, "ExternalInput")
    cb_d = dram("cb", [128, CB_W], F32, "ExternalInput")
    outT = dram("outT", [D, S], F32, "ExternalOutput")

    xT_s = dram("xT_s", [D, S], F32, "Internal")
    qT_s = dram("qT_s", [KC, 128, S], BF16, "Internal")
    kT_s = dram("kT_s", [KC, 128, S], BF16, "Internal")
    v_s = dram("v_s", [S, D], BF16, "Internal")
    oT_s = dram("oT_s", [KC, 128, S], BF16, "Internal")
    h2T_s = dram("h2T_s", [KC, 128, S], BF16, "Internal")
    fl_s = dram("fl_s", [N_FOX, S], F32, "Internal")
    faq_s = dram("faq_s", [N_FOX, 6, S], BF16, "Internal")
    fak_s = dram("fak_s", [N_FOX, 6, S], BF16, "Internal")

    with contextlib.ExitStack() as gs:
        sch = Sched(nc, gs)

        uniq = [0]

        def sb(stack, name, shape, dt):
            uniq[0] += 1
            name = "%s_%d" % (name, uniq[0])
            t = stack.enter_context(nc.sbuf_tensor(name, shape, dt))
            return t, Buf(name)

        def sbn(stack, name, shape, dt, n):
            return [sb(stack, "%s%d" % (name, i), shape, dt) for i in range(n)]

        ps = []
        for i in range(8):
            t = gs.enter_context(nc.psum_tensor("ps%d" % i, [128, 512], F32))
            ps.append((t, Buf("ps%d" % i)))

        cbt, cbb = sb(gs, "cbt", [128, CB_W], BF16)
        sch.dma("pool", out=cbt[:, :], in_=cb_d[:, :], writes=[cbb])
        negI = cbt[:, CB_NI:CB_NI + 896]
        negS = cbt[:, CB_NS:CB_NS + 896]
        ident = cbt[:, CB_ID:CB_ID + 128]
        ntri_i = cbt[:, CB_TI:CB_TI + 128]
        ntri_s = cbt[:, CB_TS:CB_TS + 128]
        ones_bf = cbt[:, CB_ON:CB_ON + 128]

        prm, prmb = sb(gs, "prm", [128, 256], F32)
        def pc(l, off, n=1):
            return prm[:, 96 * l + off: 96 * l + off + n]

        sch.dma("sp", out=prm[:, 192:200], in_=gf_d[:, :], writes=[prmb])
        sch.dma("sp", out=prm[:, 200:208], in_=c_col_d[:, :], writes=[prmb])
        for l in range(L):
            sch.dma("sp", out=pc(l, 64, 8), in_=g1_d[l], writes=[prmb])
            sch.dma("sp", out=pc(l, 72, 8), in_=g2_d[l], writes=[prmb])
            sch.dma("sp", out=prm[0:64, 96 * l + 80: 96 * l + 96], in_=gout_d[l], writes=[prmb])
            sch.dma("sp", out=prm[0:N_FOX, 216 + l: 217 + l], in_=bf_d[l], writes=[prmb])
            sch.dma("sp", out=pc(l, 0, 48), in_=bmod_d[l], writes=[prmb])

        with contextlib.ExitStack() as st:
            wm = sbn(st, "wm", [128, KC, 512], F32, 2)
            sch.op("act", lambda e: e.activation(out=prm[:, 224:232], in_=prm[:, 200:208], func=AF.Exp, scale=-1.0),
                   reads=[prmb], writes=[prmb])
            sch.op("dve", lambda e: e.tensor_scalar(out=prm[:, 224:232], in0=prm[:, 224:232], scalar1=1.0, scalar2=None, op0=ALU.add),
                   reads=[prmb], writes=[prmb])
            sch.op("dve", lambda e: e.reciprocal(out=prm[:, 232:240], in_=prm[:, 224:232]), reads=[prmb], writes=[prmb])
            sch.op("dve", lambda e: e.tensor_tensor(out=prm[:, 208:216], in0=prm[:, 200:208], in1=prm[:, 232:240], op=ALU.mult),
                   reads=[prmb], writes=[prmb])
            for l in range(L):
                sch.op("dve", lambda e: e.tensor_scalar(out=prm[0:N_FOX, 216 + l:217 + l], in0=prm[0:N_FOX, 216 + l:217 + l],
                                                        scalar1=-1.0, scalar2=None, op0=ALU.mult), reads=[prmb], writes=[prmb])
            mps, mpsb = ps[7]
            blk = 0
            for l in range(L):
                for cbk in range(12):
                    wt, wb = wm[blk % 2]
                    blk += 1
                    sch.dma("sp", out=wt[:, :, :],
                            in_=w_mod_d[l].rearrange("(k p) c -> p k c", p=128)[:, :, cbk * 512:(cbk + 1) * 512], writes=[wb])
                    for jj in range(4):
                        j = cbk * 4 + jj
                        for k in range(KC):
                            sch.op("pe", lambda e: e.matmul(mps[:, l * 48 + j: l * 48 + j + 1], lhsT=wt[:, k, jj * 128:(jj + 1) * 128],
                                                            rhs=prm[:, 208 + k: 209 + k], start=(k == 0), stop=(k == KC - 1)),
                                   reads=[wb, prmb], writes=[mpsb])
                sch.op("dve", lambda e: e.tensor_tensor(out=pc(l, 0, 48), in0=mps[:, l * 48:(l + 1) * 48], in1=pc(l, 0, 48), op=ALU.add),
                       reads=[mpsb, prmb], writes=[prmb])
                sch.op("dve", lambda e: e.scalar_tensor_tensor(out=pc(l, 48, 8), in0=pc(l, 8, 8), scalar=1.0, in1=pc(l, 64, 8),
                                                               op0=ALU.add, op1=ALU.mult), reads=[prmb], writes=[prmb])
                sch.op("dve", lambda e: e.scalar_tensor_tensor(out=pc(l, 56, 8), in0=pc(l, 32, 8), scalar=1.0, in1=pc(l, 72, 8),
                                                               op0=ALU.add, op1=ALU.mult), reads=[prmb], writes=[prmb])
            sch.barrier()
            sch.release(keep=(cbb, prmb))

        def rms_rstd(xt, xtb, sq, sqb, ssp, sspb, lnv, lnvb, rstd, rstdb):
            sch.op("act", lambda e: e.activation(out=sq[:, :, :], in_=xt[:, :, :], func=AF.Square), reads=[xtb], writes=[sqb])
            for k in range(KC):
                sch.op("pe", lambda e: e.matmul(ssp[:, :], lhsT=ones_bf, rhs=sq[:, k, :], start=(k == 0), stop=(k == KC - 1)),
                       reads=[sqb, cbb], writes=[sspb])
            sch.op("act", lambda e: e.activation(out=lnv[:, :], in_=ssp[:, :], func=AF.Ln, scale=1.0 / D, bias=EPS),
                   reads=[sspb], writes=[lnvb])
            sch.op("act", lambda e: e.activation(out=rstd[:, :], in_=lnv[:, :], func=AF.Exp, scale=-0.5), reads=[lnvb], writes=[rstdb])

        sch.op("dve", lambda e: e.memset(prm[:, 240:241], EPS), writes=[prmb])
        sch.op("dve", lambda e: e.memset(prm[:, 241:242], 1.0), writes=[prmb])
        sch.op("dve", lambda e: e.memset(prm[:, 242:243], 0.0), writes=[prmb])
        eps_col = prm[:, 240:241]
        one_col = prm[:, 241:242]
        zero_col = prm[:, 242:243]

        def xview(ap_ds):
            return ap_ds.rearrange("(k p) s -> p k s", p=128)

        def load_w_cast(dst, src_rows_ap, name, ncols, nk=KC, bw=512):
            bufs = []
            c0 = 0
            while c0 < ncols:
                c1 = min(ncols, c0 + bw)
                b = Buf("%s_c%d" % (name, c0))
                sch.dma("pool", out=dst[:, :, c0:c1], in_=src_rows_ap.rearrange("(k p) c -> p k c", p=128)[:, :, c0:c1], writes=[b])
                bufs.append(b)
                c0 = c1
            return bufs

        for l in range(L):
            xsrc = xT_in if l == 0 else xT_s
            with contextlib.ExitStack() as st:
                win, winb = sb(st, "win", [128, KC, W_IN_COLS], BF16)
                xts = sbn(st, "xt", [128, KC, T], F32, 2)
                sq, sqb = sb(st, "sq", [128, KC, T], BF16)
                tmps = sbn(st, "tmp", [128, T], F32, 2)
                lnv, lnvb = sb(st, "lnv", [128, T], F32)
                rstd, rstdb = sb(st, "rstd", [128, T], F32)
                hTs = sbn(st, "hT", [128, KC, T], BF16, 2)
                qk, qkb = sb(st, "qk", [128, 16, T], BF16)
                vsb, vsbb = sb(st, "vsb", [128, 4, D], BF16)
                fe, feb = sb(st, "fe", [N_FOX, T], F32)
                fsp, fspb = sb(st, "fsp", [N_FOX, T], F32)
                winbs = load_w_cast(win, w_in_d[l], "win", W_IN_COLS)

                def loadx(t):
                    xt, xtb = xts[t % 2]
                    sch.dma("sp", out=xt[:, :, :], in_=xview(xsrc)[:, :, t * T:(t + 1) * T], writes=[xtb])

                loadx(0)
                nev = 0
                for t in range(NT):
                    if t + 1 < NT:
                        loadx(t + 1)
                    xt, xtb = xts[t % 2]
                    hT, hTb = hTs[t % 2]
                    ssp, sspb = ps[4 + t % 2]
                    rms_rstd(xt, xtb, sq, sqb, ssp, sspb, lnv, lnvb, rstd, rstdb)
                    for k in range(KC):
                        tm, tmb = tmps[k % 2]
                        sch.op("dve", lambda e: e.tensor_tensor(out=tm[:, :], in0=xt[:, k, :], in1=rstd[:, :], op=ALU.mult),
                               reads=[xtb, rstdb], writes=[tmb])
                        sch.op("act", lambda e: e.activation(out=hT[:, k, :], in_=tm[:, :], func=AF.Identity,
                                                             bias=pc(l, 0 + k), scale=pc(l, 48 + k)),
                               reads=[tmb, prmb], writes=[hTb])
                    for c in range(16):
                        pp, ppb = ps[c % 4]
                        for k in range(KC):
                            sch.op("pe", lambda e: e.matmul(pp[:, :], lhsT=win[:, k, c * 128:(c + 1) * 128], rhs=hT[:, k, :],
                                                            start=(k == 0), stop=(k == KC - 1)), reads=[winbs[c // 4], hTb], writes=[ppb])
                        if nev % 2 == 0:
                            sch.op("act", lambda e: e.copy(out=qk[:, c, :], in_=pp[:, :]), reads=[ppb], writes=[qkb])
                        else:
                            sch.op("dve", lambda e: e.tensor_copy(out=qk[:, c, :], in_=pp[:, :]), reads=[ppb], writes=[qkb])
                        nev += 1
                    sch.dma("pool", out=qT_s.rearrange("k p s -> p k s")[:, :, t * T:(t + 1) * T], in_=qk[:, 0:8, :], reads=[qkb])
                    sch.dma("pool", out=kT_s.rearrange("k p s -> p k s")[:, :, t * T:(t + 1) * T], in_=qk[:, 8:16, :], reads=[qkb])
                    for tb in range(4):
                        for half in range(2):
                            pp, ppb = ps[nev % 4]
                            for k in range(KC):
                                sch.op("pe", lambda e: e.matmul(pp[:, :], lhsT=hT[:, k, tb * 128:(tb + 1) * 128],
                                                                rhs=win[:, k, 2 * D + half * 512: 2 * D + (half + 1) * 512],
                                                                start=(k == 0), stop=(k == KC - 1)), reads=[winbs[4 + half], hTb], writes=[ppb])
                            if nev % 2 == 0:
                                sch.op("act", lambda e: e.copy(out=vsb[:, tb, half * 512:(half + 1) * 512], in_=pp[:, :]),
                                       reads=[ppb], writes=[vsbb])
                            else:
                                sch.op("dve", lambda e: e.tensor_copy(out=vsb[:, tb, half * 512:(half + 1) * 512], in_=pp[:, :]),
                                       reads=[ppb], writes=[vsbb])
                            nev += 1
                    sch.dma("pool", out=v_s.rearrange("(n p) c -> p n c", p=128)[:, t * 4:(t + 1) * 4, :], in_=vsb[:, :, :], reads=[vsbb])
                    fp, fpb = ps[6]
                    for k in range(KC):
                        sch.op("pe", lambda e: e.matmul(fp[0:N_FOX, :], lhsT=win[:, k, 3 * D:3 * D + N_FOX], rhs=hT[:, k, :],
                                                        start=(k == 0), stop=(k == KC - 1)), reads=[winbs[6], hTb], writes=[fpb])
                    sch.op("act", lambda e: e.activation(out=fe[:, :], in_=fp[0:N_FOX, :], func=AF.Exp, scale=-1.0,
                                                         bias=prm[0:N_FOX, 216 + l:217 + l]), reads=[fpb, prmb], writes=[feb])
                    sch.op("act", lambda e: e.activation(out=fsp[:, :], in_=fe[:, :], func=AF.Ln, bias=one_col[0:N_FOX, :]),
                           reads=[feb, prmb], writes=[fspb])
                    sch.dma("pool", out=fl_s[:, t * T:(t + 1) * T], in_=fsp[:, :], reads=[fspb])
                sch.barrier()
                sch.release(keep=(cbb, prmb))

            with contextlib.ExitStack() as st:
                CH = min(S, 2048)
                onesf, onesfb = sb(st, "onesf", [N_FOX, CH], F32)
                ones3, ones3b = sb(st, "ones3", [N_FOX, 3, CH], BF16)
                fls = sbn(st, "fl", [N_FOX, CH], F32, 2)
                G, Gb = sb(st, "G", [N_FOX, CH], F32)
                G8, G8b = sb(st, "G8", [N_FOX, CH], F32)
                r1, r1b = sb(st, "r1", [N_FOX, CH], F32)
                cpos = sbn(st, "cpos", [N_FOX, 3, CH], BF16, 2)
                cneg = sbn(st, "cneg", [N_FOX, 3, CH], BF16, 2)
                carry, carryb = sb(st, "carry", [N_FOX, 2], F32)
                sch.op("dve", lambda e: e.memset(onesf[:, :], 1.0), writes=[onesfb])
                sch.op("dve", lambda e: e.memset(ones3[:, :, :], 1.0), writes=[ones3b])
                sch.op("dve", lambda e: e.memset(carry[:, :], 0.0), writes=[carryb])
                for ci in range(S // CH):
                    fl, flb = fls[ci % 2]
                    cp, cpb = cpos[ci % 2]
                    cn, cnb = cneg[ci % 2]
                    sl = slice(ci * CH, (ci + 1) * CH)
                    sch.dma("sp", out=fl[:, :], in_=fl_s[:, sl], writes=[flb])
                    sch.op("dve", lambda e: e.tensor_tensor_scan(out=G[:, :], data0=onesf[:, :], data1=fl[:, :], initial=carry[:, 0:1],
                                                                 op0=ALU.mult, op1=ALU.add), reads=[onesfb, flb, carryb], writes=[Gb])
                    sch.op("dve", lambda e: e.tensor_copy(out=carry[:, 0:1], in_=G[:, CH - 1:CH]), reads=[Gb], writes=[carryb])
                    sch.op("dve", lambda e: e.tensor_scalar(out=G8[:, :], in0=G[:, :], scalar1=8.0, scalar2=None, op0=ALU.mult),
                           reads=[Gb], writes=[G8b])
                    sch.op("dve", lambda e: e.tensor_copy(out=cp[:, 0, :], in_=G8[:, :]), reads=[G8b], writes=[cpb])
                    sch.op("dve", lambda e: e.tensor_tensor(out=r1[:, :], in0=G8[:, :], in1=cp[:, 0, :], op=ALU.subtract),
                           reads=[G8b, cpb], writes=[r1b])
                    sch.op("dve", lambda e: e.tensor_copy(out=cp[:, 1, :], in_=r1[:, :]), reads=[r1b], writes=[cpb])
                    sch.op("dve", lambda e: e.tensor_tensor(out=G8[:, :], in0=r1[:, :], in1=cp[:, 1, :], op=ALU.subtract),
                           reads=[r1b, cpb], writes=[G8b])
                    sch.op("dve", lambda e: e.tensor_copy(out=cp[:, 2, :], in_=G8[:, :]), reads=[G8b], writes=[cpb])
                    sch.op("dve", lambda e: e.tensor_scalar(out=cn[:, :, :], in0=cp[:, :, :], scalar1=-1.0, scalar2=None, op0=ALU.mult),
                           reads=[cpb], writes=[cnb])
                    sch.dma("pool", out=fak_s[:, 0:3, sl], in_=cp[:, :, :], reads=[cpb])
                    sch.dma("pool", out=fak_s[:, 3:6, sl], in_=ones3[:, :, :], reads=[ones3b])
                    sch.dma("pool", out=faq_s[:, 0:3, sl], in_=ones3[:, :, :], reads=[ones3b])
                    sch.dma("pool", out=faq_s[:, 3:6, sl], in_=cn[:, :, :], reads=[cnb])
                sch.barrier()
                sch.release(keep=(cbb, prmb))

            with contextlib.ExitStack() as st:
                qTs = sbn(st, "qT", [70, S], BF16, 2)
                kTs = sbn(st, "kT", [70, S], BF16, 2)
                vs = sbn(st, "v", [128, NB, 65], BF16, 2)
                mts = sbn(st, "mt", [128, 23 * 128], F32, 2)
                Es = sbn(st, "E", [128, T], F32, 3)
                E2s = sbn(st, "E2", [128, 2 * T], F32, 3)
                Lps = sbn(st, "Lp", [128, 2 * T], BF16, 3)
                EGs = sbn(st, "EG", [128, T], F32, 2)
                pTs = sbn(st, "pT", [128, T], BF16, 4)
                sqs = sbn(st, "sqo", [65, T], BF16, 2)
                lns = sbn(st, "lno", [64, T], F32, 2)
                rrs = sbn(st, "rro", [64, T], F32, 2)
                osb = sbn(st, "osb", [64, T], BF16, 2)
                wss = cbt[:, CB_WS:CB_WS + 64]
                for s_ in range(2):
                    vt, vb = vs[s_]
                    sch.op("dve", lambda e: e.memset(vt[:, :, 64:65], 1.0), writes=[vb])
                    sch.op("dve", lambda e: e.memset(qTs[s_][0][64:70, :], 0.0), writes=[qTs[s_][1]])
                    sch.op("dve", lambda e: e.memset(kTs[s_][0][64:70, :], 0.0), writes=[kTs[s_][1]])

                def load_head(h):
                    s_ = h % 2
                    qt, qb = qTs[s_]
                    kt, kb = kTs[s_]
                    vt, vb = vs[s_]
                    r0 = (h % 2) * 64
                    sch.dma("sp", out=qt[0:64, :], in_=qT_s[h // 2, r0:r0 + 64, :], writes=[qb])
                    sch.dma("sp", out=kt[0:64, :], in_=kT_s[h // 2, r0:r0 + 64, :], writes=[kb])
                    if h >= N_DIL + N_FOX:
                        sch.op("dve", lambda e: e.memset(kt[64:70, :], 0.0), writes=[kb])
                    if N_DIL <= h < N_DIL + N_FOX:
                        hf = h - N_DIL
                        sch.dma("sp", out=qt[64:70, :], in_=faq_s[hf], writes=[qb])
                        sch.dma("sp", out=kt[64:70, :], in_=fak_s[hf], writes=[kb])
                    sch.dma("sp", out=vt[:, :, 0:64], in_=v_s.rearrange("(n p) c -> p n c", p=128)[:, :, h * 64:(h + 1) * 64],
                            writes=[vb])
                    if h < N_DIL:
                        mt, mb = mts[s_]
                        sch.dma("sp", out=mt[:, :], in_=mt_d[h], writes=[mb])

                epi_n = [0]

                def attn_head(h):
                    s_ = h % 2
                    kind = "dil" if h < N_DIL else ("fox" if h < N_DIL + N_FOX else "sb")
                    K = 70
                    qt, qb = qTs[s_]
                    kt, kb = kTs[s_]
                    vt, vb = vs[s_]
                    mt, mb = mts[s_]
                    pairs = []
                    for i in range(NT):
                        if kind == "dil":
                            ms = [m for m in range(4 * i - 16, 4 * i + 4) if m >= 0]
                        elif kind == "fox":
                            ms = list(range(0, 4 * i + 4))
                        else:
                            ms = list(range(4 * i + 3, -1, -1))
                        for idx, m in enumerate(ms):
                            pairs.append((i, m, idx == 0, idx == len(ms) - 1))
                    n = len(pairs)

                    def stage1(pi):
                        i, m, first, last = pairs[pi]
                        stt, stb = ps[pi % 3]
                        diag = (m >= 4 * i) and kind != "dil"
                        sch.op("pe", lambda e: e.matmul(stt[:, :], lhsT=kt[0:K, m * 128:(m + 1) * 128], rhs=qt[0:K, i * T:(i + 1) * T],
                                                        start=True, stop=not diag), reads=[kb, qb], writes=[stb])
                        if diag:
                            r = m - 4 * i
                            tab = negI if kind == "fox" else negS
                            sch.op("pe", lambda e: e.matmul(stt[:, :], lhsT=ident, rhs=tab[:, 384 - 128 * r: 896 - 128 * r],
                                                            start=False, stop=True), reads=[cbb], writes=[stb])

                    def epilogue(i):
                        ot, otb = ps[5 + i % 2]
                        mst, msb = ps[7]
                        en = epi_n[0]
                        epi_n[0] += 1
                        sqt, sqb_ = sqs[en % 2]
                        lnt, lnb = lns[en % 2]
                        rrt, rrb = rrs[en % 2]
                        ost, osb_ = osb[en % 2]
                        R = 65
                        wcol = CB_WS2 if kind == "sb" else CB_WS
                        bval = EPS if kind == "sb" else 0.0
                        sch.op("act", lambda e: e.activation(out=sqt[0:R, :], in_=ot[0:R, :], func=AF.Square), reads=[otb], writes=[sqb_])
                        sch.op("pe", lambda e: e.matmul(mst[:, :], lhsT=cbt[0:R, wcol:wcol + 128], rhs=sqt[0:R, :], start=True, stop=True),
                               reads=[sqb_, cbb], writes=[msb])
                        sch.op("act", lambda e: e.activation(out=lnt[:, :], in_=mst[0:64, :], func=AF.Ln, bias=bval),
                               reads=[msb], writes=[lnb])
                        sch.op("act", lambda e: e.activation(out=rrt[:, :], in_=lnt[:, :], func=AF.Exp, scale=-0.5), reads=[lnb], writes=[rrb])
                        sch.op("dve", lambda e: e.scalar_tensor_tensor(out=ost[:, :], in0=ot[0:64, :], scalar=prm[0:64, 96 * l + 80 + h: 96 * l + 81 + h],
                                                                       in1=rrt[:, :], op0=ALU.mult, op1=ALU.mult),
                               reads=[otb, rrb, prmb], writes=[osb_])
                        r0 = (h % 2) * 64
                        sch.dma("pool", out=oT_s[h // 2, r0:r0 + 64, i * T:(i + 1) * T], in_=ost[:, :], reads=[osb_])

                    def s2_act(pi):
                        i, m, first, last = pairs[pi]
                        stt, stb = ps[pi % 3]
                        pt, pb = pTs[pi % 4]
                        if kind == "fox":
                            sch.op("act", lambda e: e.activation(out=pt[:, :], in_=stt[:, :], func=AF.Exp, scale=0.125), reads=[stb], writes=[pb])
                        elif kind == "dil":
                            Et, Eb = Es[pi % 3]
                            d0 = 4 * i - m
                            sch.op("act", lambda e: e.activation(out=Et[:, :], in_=stt[:, :], func=AF.Exp, scale=0.125), reads=[stb], writes=[Eb])
                            meng = "pool" if pi % 3 == 2 else "dve"
                            sch.op(meng, lambda e: e.tensor_tensor(out=pt[:, :], in0=Et[:, :], in1=mt[:, (d0 + 3) * 128:(d0 + 7) * 128], op=ALU.mult),
                                   reads=[Eb, mb], writes=[pb])
                        else:
                            Et, Eb = E2s[(pi // 2) % 3]
                            sch.op("act", lambda e: e.activation(out=Et[:, (pi % 2) * T:(pi % 2 + 1) * T], in_=stt[:, :], func=AF.Exp, scale=0.125),
                                   reads=[stb], writes=[Eb])
                            if pi % 2 == 1:
                                Lt, Lb = Lps[(pi // 2) % 3]
                                sch.op("act", lambda e: e.activation(out=Lt[:, :], in_=Et[:, :], func=AF.Ln, bias=1.0), reads=[Eb], writes=[Lb])

                    def s3_tri1(pi):
                        i, m, first, last = pairs[pi]
                        Lt, Lb = Lps[(pi // 2) % 3]
                        hc = slice((pi % 2) * T, (pi % 2 + 1) * T)
                        act_, accb = ps[3 + i % 2]
                        sch.op("pe", lambda e: e.matmul(act_[:, :], lhsT=ntri_i, rhs=Lt[:, hc], start=first, stop=True, skip_group_check=not first),
                               reads=[Lb, cbb], writes=[accb])

                    def s3_eg(pi):
                        i, m, first, last = pairs[pi]
                        pt, pb = pTs[pi % 4]
                        Et, Eb = E2s[(pi // 2) % 3]
                        hc = slice((pi % 2) * T, (pi % 2 + 1) * T)
                        Gt, Gb_ = EGs[pi % 2]
                        act_, accb = ps[3 + i % 2]
                        sch.op("act", lambda e: e.activation(out=Gt[:, :], in_=act_[:, :], func=AF.Exp), reads=[accb], writes=[Gb_])
                        sch.op("dve", lambda e: e.tensor_tensor(out=pt[:, :], in0=Et[:, hc], in1=Gt[:, :], op=ALU.mult),
                               reads=[Eb, Gb_], writes=[pb])

                    def s3_tri2(pi):
                        i, m, first, last = pairs[pi]
                        if last:
                            return
                        Lt, Lb = Lps[(pi // 2) % 3]
                        hc = slice((pi % 2) * T, (pi % 2 + 1) * T)
                        act_, accb = ps[3 + i % 2]
                        sch.op("pe", lambda e: e.matmul(act_[:, :], lhsT=ntri_s, rhs=Lt[:, hc], start=False, stop=True, skip_group_check=True),
                               reads=[Lb, cbb], writes=[accb])

                    def s4_pv(pi):
                        i, m, first, last = pairs[pi]
                        ot, otb = ps[5 + i % 2]
                        pt, pb = pTs[pi % 4]
                        R = 65
                        sch.op("pe", lambda e: e.matmul(ot[0:R, :], lhsT=vt[:, m, 0:R], rhs=pt[:, :], start=first, stop=last),
                               reads=[vb, pb], writes=[otb])
                        if last:
                            epilogue(i)

                    if kind == "sb":
                        stages = [(s3_tri2, 4), (s3_tri1, 3), (stage1, 0), (s2_act, 1), (s3_eg, 3), (s4_pv, 4)]
                    else:
                        stages = [(stage1, 0), (s2_act, 2), (s4_pv, 3)]
                    depth = max(lag for _, lag in stages)
                    for idx in range(n + depth):
                        for fn, lag in stages:
                            p = idx - lag
                            if 0 <= p < n:
                                fn(p)

                load_head(0)
                for h in range(NH):
                    if h + 1 < NH:
                        load_head(h + 1)
                    attn_head(h)
                sch.barrier()
                sch.release(keep=(cbb, prmb))

            with contextlib.ExitStack() as st:
                wo, wob = sb(st, "wo", [128, KC, D], BF16)
                xts = sbn(st, "xt", [128, KC, T], F32, 2)
                oTs = sbn(st, "oTt", [128, KC, T], BF16, 2)
                sq, sqb = sb(st, "sq", [128, KC, T], BF16)
                tmps = sbn(st, "tmp", [128, T], F32, 2)
                lnv, lnvb = sb(st, "lnv", [128, T], F32)
                rstd, rstdb = sb(st, "rstd", [128, T], F32)
                hTs = sbn(st, "hT", [128, KC, T], BF16, 2)
                wobs = load_w_cast(wo, w_out_d[l], "wo", D)

                def loadc1(t):
                    xt, xtb = xts[t % 2]
                    ott, otb = oTs[t % 2]
                    sch.dma("sp", out=xt[:, :, :], in_=xview(xsrc)[:, :, t * T:(t + 1) * T], writes=[xtb])
                    sch.dma("sp", out=ott[:, :, :], in_=oT_s.rearrange("k p s -> p k s")[:, :, t * T:(t + 1) * T], writes=[otb])

                loadc1(0)
                for t in range(NT):
                    if t + 1 < NT:
                        loadc1(t + 1)
                    xt, xtb = xts[t % 2]
                    ott, otb = oTs[t % 2]
                    hT, hTb = hTs[t % 2]
                    for c in range(KC):
                        pp, ppb = ps[c % 4]
                        for k in range(KC):
                            sch.op("pe", lambda e: e.matmul(pp[:, :], lhsT=wo[:, k, c * 128:(c + 1) * 128], rhs=ott[:, k, :],
                                                            start=(k == 0), stop=(k == KC - 1)), reads=[wobs[c // 4], otb], writes=[ppb])
                        sch.op("dve", lambda e: e.scalar_tensor_tensor(out=xt[:, c, :], in0=pp[:, :], scalar=pc(l, 16 + c), in1=xt[:, c, :],
                                                                       op0=ALU.mult, op1=ALU.add), reads=[ppb, xtb, prmb], writes=[xtb])
                    ssp, sspb = ps[4 + t % 2]
                    rms_rstd(xt, xtb, sq, sqb, ssp, sspb, lnv, lnvb, rstd, rstdb)
                    for k in range(KC):
                        tm, tmb = tmps[k % 2]
                        sch.op("dve", lambda e: e.tensor_tensor(out=tm[:, :], in0=xt[:, k, :], in1=rstd[:, :], op=ALU.mult),
                               reads=[xtb, rstdb], writes=[tmb])
                        sch.op("act", lambda e: e.activation(out=hT[:, k, :], in_=tm[:, :], func=AF.Identity,
                                                             bias=pc(l, 24 + k), scale=pc(l, 56 + k)),
                               reads=[tmb, prmb], writes=[hTb])
                    sch.dma("pool", out=xview(xT_s)[:, :, t * T:(t + 1) * T], in_=xt[:, :, :], reads=[xtb])
                    sch.dma("pool", out=h2T_s.rearrange("k p s -> p k s")[:, :, t * T:(t + 1) * T], in_=hT[:, :, :], reads=[hTb])
                sch.barrier()
                sch.release(keep=(cbb, prmb))

            for half in range(2):
                last_phase = (l == L - 1 and half == 1)
                with contextlib.ExitStack() as st:
                    HH = DFF // 2
                    w1t, w1b = sb(st, "w1t", [128, KC, HH], BF16)
                    w2t, w2b = sb(st, "w2t", [128, HH // 128, D], BF16)
                    xts = sbn(st, "xt", [128, KC, T], F32, 2)
                    hTs = sbn(st, "hT", [128, KC, T], BF16, 2)
                    rbs = sbn(st, "rb", [128, T], F32, 2)
                    hid, hidb = sb(st, "hid", [128, HH // 128, T], BF16)
                    if last_phase:
                        sq, sqb = sb(st, "sq", [128, KC, T], BF16)
                        lnv, lnvb = sb(st, "lnv", [128, T], F32)
                        rstd, rstdb = sb(st, "rstd", [128, T], F32)
                        outs = sbn(st, "ot", [128, KC, T], F32, 1)
                    w1bs = load_w_cast(w1t, w1_d[l][:, half * HH:(half + 1) * HH], "w1", HH)
                    w2bs = load_w_cast(w2t, w2_d[l][half * HH:(half + 1) * HH, :], "w2", D, nk=HH // 128, bw=256)

                    def loadc2(t):
                        xt, xtb = xts[t % 2]
                        hT, hTb = hTs[t % 2]
                        sch.dma("sp", out=xt[:, :, :], in_=xview(xT_s)[:, :, t * T:(t + 1) * T], writes=[xtb])
                        sch.dma("sp", out=hT[:, :, :], in_=h2T_s.rearrange("k p s -> p k s")[:, :, t * T:(t + 1) * T], writes=[hTb])

                    loadc2(0)
                    for t in range(NT):
                        if t + 1 < NT:
                            loadc2(t + 1)
                        xt, xtb = xts[t % 2]
                        hT, hTb = hTs[t % 2]
                        for j in range(HH // 128):
                            pp, ppb = ps[j % 4]
                            rb, rbb = rbs[j % 2]
                            for k in range(KC):
                                sch.op("pe", lambda e: e.matmul(pp[:, :], lhsT=w1t[:, k, j * 128:(j + 1) * 128], rhs=hT[:, k, :],
                                                                start=(k == 0), stop=(k == KC - 1)), reads=[w1bs[j // 4], hTb], writes=[ppb])
                            sch.op("act", lambda e: e.activation(out=rb[:, :], in_=pp[:, :], func=AF.Relu), reads=[ppb], writes=[rbb])
                            sch.op("dve", lambda e: e.scalar_tensor_tensor(out=hid[:, j, :], in0=pp[:, :], scalar=0.0, in1=rb[:, :],
                                                                           op0=ALU.max, op1=ALU.mult), reads=[ppb, rbb], writes=[hidb])
                        for c in range(KC):
                            pp, ppb = ps[4 + c % 4]
                            nj = HH // 128
                            for j in range(nj):
                                sch.op("pe", lambda e: e.matmul(pp[:, :], lhsT=w2t[:, j, c * 128:(c + 1) * 128], rhs=hid[:, j, :],
                                                                start=(j == 0), stop=(j == nj - 1)), reads=[w2bs[c // 2], hidb], writes=[ppb])
                            sch.op("dve", lambda e: e.scalar_tensor_tensor(out=xt[:, c, :], in0=pp[:, :], scalar=pc(l, 40 + c), in1=xt[:, c, :],
                                                                           op0=ALU.mult, op1=ALU.add), reads=[ppb, xtb, prmb], writes=[xtb])
                        if not last_phase:
                            sch.dma("pool", out=xview(xT_s)[:, :, t * T:(t + 1) * T], in_=xt[:, :, :], reads=[xtb])
                        else:
                            ot, otb = outs[0]
                            ssp, sspb = ps[0]
                            rms_rstd(xt, xtb, sq, sqb, ssp, sspb, lnv, lnvb, rstd, rstdb)
                            for k in range(KC):
                                sch.op("dve", lambda e: e.scalar_tensor_tensor(out=ot[:, k, :], in0=xt[:, k, :], scalar=prm[:, 192 + k:193 + k],
                                                                               in1=rstd[:, :], op0=ALU.mult, op1=ALU.mult),
                                       reads=[xtb, rstdb, prmb], writes=[otb])
                            sch.dma("pool", out=xview(outT)[:, :, t * T:(t + 1) * T], in_=ot[:, :, :], reads=[otb])
                    sch.barrier()
                    sch.release(keep=(cbb, prmb))
                sch.release(keep=(cbb, prmb))
        sch.barrier()
        print("instructions emitted:", sch.ninst)
    return nc


_CACHE = {}


def kernel(x, c, w_mod, b_mod, g_norm1, w_in, b_f, g_out, w_out, g_norm2, w_mlp_in, w_mlp_out, g_final):
    x = np.asarray(x, np.float32)
    B, S, _ = x.shape
    L = int(np.asarray(w_mod).shape[0])
    assert B == N_CORES
    key = (S, L)
    if key not in _CACHE:
        _CACHE[key] = build(S, L)
    nc = _CACHE[key]
    mt, cb = _const_tables()
    f = lambda a: np.ascontiguousarray(np.asarray(a, np.float32))
    col = lambda a: np.ascontiguousarray(np.asarray(a, np.float32).reshape(-1, 128).T)
    shared = {
        "w_mod": f(w_mod),
        "bmod_col": np.stack([col(np.asarray(b_mod)[l]) for l in range(L)]),
        "g1_col": np.stack([col(np.asarray(g_norm1)[l]) for l in range(L)]),
        "g2_col": np.stack([col(np.asarray(g_norm2)[l]) for l in range(L)]),
        "gout_col": np.stack([np.ascontiguousarray(np.asarray(g_out, np.float32)[l].reshape(NH, HD).T) for l in range(L)]),
        "gf_col": col(g_final),
        "bf_col": f(b_f).reshape(L, N_FOX, 1),
        "w_in": f(w_in), "w_out": f(w_out), "w1": f(w_mlp_in), "w2": f(w_mlp_out),
        "mtab": mt, "cb": cb,
    }
    c = np.asarray(c, np.float32)
    in_maps = []
    for b in range(B):
        m = dict(shared)
        m["xT"] = np.ascontiguousarray(x[b].T)
        m["c_col"] = col(c[b])
        in_maps.append(m)
    res = run_bass_kernel_spmd(nc, in_maps, core_ids=list(range(N_CORES)))
    out = np.stack([np.ascontiguousarray(res.results[b]["outT"].T) for b in range(B)], axis=0)
    return out.astype(np.float32)
```

```python
import contextlib
import numpy as np
import concourse.bass as bass
import concourse.mybir as mybir
from concourse.bass_utils import run_bass_kernel_spmd

F32 = mybir.dt.float32
BF16 = mybir.dt.bfloat16
AF = mybir.ActivationFunctionType
ALU = mybir.AluOpType

D = 1024
KC = 8
T = 512
NH = 16
HD = 64
N_DIL, N_FOX, N_SB = 6, 5, 5
DFF = 4096
W_IN_COLS = 3 * D + N_FOX
EPS = 1e-6
NEG = -1.0e9
N_CORES = 8


class Buf:
    __slots__ = ("name", "last_write", "readers", "ds")

    def __init__(self, name):
        self.name = name
        self.last_write = None
        self.readers = []
        self.ds = {}


class Sched:
    def __init__(self, nc, stack):
        self.nc = nc
        self.stack = stack
        self.E = {"pe": nc.tensor, "act": nc.scalar, "dve": nc.vector, "pool": nc.gpsimd, "sp": nc.sync}
        self.sem = {}
        self.cnt = {}
        for e in ("pe", "act", "dve", "pool"):
            self.sem[e] = stack.enter_context(nc.semaphore("s_" + e))
            self.cnt[e] = 0
        self.seen = {e: {} for e in self.E}
        self.ninst = 0
        self.dma_bufs = []
        self.free_sems = {"hw": [], "sw": []}
        self.nsem = 0

    def _wait(self, e, rec):
        _, sem, val = rec
        d = self.seen[e]
        k = id(sem)
        if d.get(k, 0) >= val:
            return
        d[k] = val
        self.E[e].wait_ge(sem, val)
        self.ninst += 1

    @staticmethod
    def _compact(readers):
        best = {}
        for r in readers:
            k = id(r[1])
            if k not in best or best[k][2] < r[2]:
                best[k] = r
        return list(best.values())

    def _finish(self, rec, reads, writes):
        for b in reads:
            b.readers.append(rec)
            if len(b.readers) > 6:
                b.readers = self._compact(b.readers)
        for b in writes:
            b.last_write = rec
            b.readers = []

    def _wait_all(self, e, recs):
        best = {}
        for r in recs:
            k = id(r[1])
            if k not in best or best[k][2] < r[2]:
                best[k] = r
        for r in best.values():
            self._wait(e, r)

    def op(self, e, fn, reads=(), writes=()):
        recs = []
        for b in reads:
            if b.last_write is not None:
                recs.append(b.last_write)
        for b in writes:
            w = b.last_write
            if w is not None and w[0] != e:
                recs.append(w)
            for r in b.readers:
                if r[0] == e:
                    continue
                recs.append(r)
        self._wait_all(e, recs)
        ins = fn(self.E[e])
        self.cnt[e] += 1
        ins.then_inc(self.sem[e], 1)
        self.ninst += 1
        self._finish((e, self.sem[e], self.cnt[e]), reads, writes)
        return ins

    def dma(self, q, out, in_, reads=(), writes=(), owner=None, **kw):
        recs = []
        for b in reads:
            if b.last_write is not None:
                recs.append(b.last_write)
        for b in writes:
            if b.last_write is not None:
                recs.append(b.last_write)
            recs.extend(b.readers)
        self._wait_all(q, recs)
        if owner is None:
            owner = (list(writes) + list(reads))[0]
        qc = "sw" if q == "pool" else "hw"
        d = owner.ds.get(qc)
        if d is None:
            if self.free_sems[qc]:
                d = self.free_sems[qc].pop()
            else:
                d = [self.stack.enter_context(self.nc.semaphore("d%s%d" % (qc, self.nsem))), 0]
                self.nsem += 1
            owner.ds[qc] = d
            if owner not in self.dma_bufs:
                self.dma_bufs.append(owner)
        d[1] += 16
        ins = self.E[q].dma_start(out=out, in_=in_, **kw)
        ins.then_inc(d[0], 16)
        self.ninst += 1
        self._finish(("dma", d[0], d[1]), reads, writes)
        return ins

    def barrier(self):
        for e in self.E:
            for e2 in ("pe", "act", "dve", "pool"):
                if self.cnt[e2] > 0 and e2 != e:
                    self._wait(e, (e2, self.sem[e2], self.cnt[e2]))
            for b in self.dma_bufs:
                for d in b.ds.values():
                    if d[1] > 0:
                        self._wait(e, ("dma", d[0], d[1]))

    def release(self, keep=()):
        kept = []
        for b in self.dma_bufs:
            if b in keep:
                kept.append(b)
            else:
                for qc, d in b.ds.items():
                    self.free_sems[qc].append(d)
                b.ds = {}
        self.dma_bufs = kept


def _const_tables():
    j = np.arange(128)[:, None]
    slopes = 2.0 ** (-8.0 * np.arange(1, N_DIL + 1, dtype=np.float64) / N_DIL)
    d = np.arange(23)[None, :, None] - 3
    i = np.arange(128)[None, None, :]
    delta = 128 * d + i - j[:, :, None]
    cnt = ((delta >= 0) & (delta <= 128)).astype(np.float64)
    cnt += ((delta >= 0) & (delta <= 512) & (delta % 4 == 0))
    cnt += ((delta >= 0) & (delta <= 2048) & (delta % 16 == 0))
    mt = np.zeros((N_DIL, 128, 23, 128), np.float64)
    for h in range(N_DIL):
        mt[h] = cnt * np.exp(-slopes[h] * np.maximum(delta, 0))
    mt = mt.reshape(N_DIL, 128, 23 * 128).astype(np.float32)
    u = np.arange(896)[None, :]
    neg_incl = np.where(j > u - 384, NEG, 0.0).astype(np.float32)
    neg_strict = np.where(j >= u - 384, NEG, 0.0).astype(np.float32)
    s = np.arange(128)[None, :]
    ident = (j == s).astype(np.float32)
    ntri_incl = np.where(j >= s, -1.0, 0.0).astype(np.float32)
    ntri_strc = np.where(j < s, -1.0, 0.0).astype(np.float32)
    ones = np.ones((128, 128), np.float32)
    wss = np.zeros((128, 64), np.float32)
    wss[0:64] = 1.0 / 64.0
    wss[64] = EPS
    wss2 = wss.copy()
    wss2[64] = 0.0
    cb = np.concatenate([neg_incl, neg_strict, ident, ntri_incl, ntri_strc, ones, np.pad(wss, ((0, 0), (0, 64))),
                         np.pad(wss2, ((0, 0), (0, 64)))], axis=1)
    return mt, np.ascontiguousarray(cb)


CB_NI, CB_NS, CB_ID, CB_TI, CB_TS, CB_ON, CB_WS = 0, 896, 1792, 1920, 2048, 2176, 2304
CB_WS2 = 2432
CB_W = 2560


def build(S, L):
    NT = S // T
    NB = S // 128
    nc = bass.Bass("TRN2", target_bir_lowering=False)

    def dram(name, shape, dt, kind):
        return nc.dram_tensor(name, shape, dt, kind=kind).ap()

    xT_in = dram("xT", [D, S], F32, "ExternalInput")
    c_col_d = dram("c_col", [128, KC], F32, "ExternalInput")
    w_mod_d = dram("w_mod", [L, D, 6 * D], F32, "ExternalInput")
    bmod_d = dram("bmod_col", [L, 128, 48], F32, "ExternalInput")
    g1_d = dram("g1_col", [L, 128, KC], F32, "ExternalInput")
    g2_d = dram("g2_col", [L, 128, KC], F32, "ExternalInput")
    gout_d = dram("gout_col", [L, 64, NH], F32, "ExternalInput")
    gf_d = dram("gf_col", [128, KC], F32, "ExternalInput")
    bf_d = dram("bf_col", [L, N_FOX, 1], F32, "ExternalInput")
    w_in_d = dram("w_in", [L, D, W_IN_COLS], F32, "ExternalInput")
    w_out_d = dram("w_out", [L, D, D], F32, "ExternalInput")
    w1_d = dram("w1", [L, D, DFF], F32, "ExternalInput")
    w2_d = dram("w2", [L, DFF, D], F32, "ExternalInput")
    mt_d = dram("mtab", [N_DIL, 128, 23 * 128], F32, "ExternalInput")
    cb_d = dram("cb", [128, CB_W], F32, "ExternalInput")
    outT = dram("outT", [D, S], F32, "ExternalOutput")

    xT_s = dram("xT_s", [D, S], F32, "Internal")
    qT_s = dram("qT_s", [KC, 128, S], BF16, "Internal")
    kT_s = dram("kT_s", [KC, 128, S], BF16, "Internal")
    v_s = dram("v_s", [S, D], BF16, "Internal")
    oT_s = dram("oT_s", [KC, 128, S], BF16, "Internal")
    h2T_s = dram("h2T_s", [KC, 128, S], BF16, "Internal")
    fl_s = dram("fl_s", [N_FOX, S], F32, "Internal")
    faq_s = dram("faq_s", [N_FOX, 6, S], BF16, "Internal")
    fak_s = dram("fak_s", [N_FOX, 6, S], BF16, "Internal")

    with contextlib.ExitStack() as gs:
        sch = Sched(nc, gs)

        uniq = [0]

        def sb(stack, name, shape, dt):
            uniq[0] += 1
            name = "%s_%d" % (name, uniq[0])
            t = stack.enter_context(nc.sbuf_tensor(name, shape, dt))
            return t, Buf(name)

        def sbn(stack, name, shape, dt, n):
            return [sb(stack, "%s%d" % (name, i), shape, dt) for i in range(n)]

        ps = []
        for i in range(8):
            t = gs.enter_context(nc.psum_tensor("ps%d" % i, [128, 512], F32))
            ps.append((t, Buf("ps%d" % i)))

        cbt, cbb = sb(gs, "cbt", [128, CB_W], BF16)
        sch.dma("pool", out=cbt[:, :], in_=cb_d[:, :], writes=[cbb])
        negI = cbt[:, CB_NI:CB_NI + 896]
        negS = cbt[:, CB_NS:CB_NS + 896]
        ident = cbt[:, CB_ID:CB_ID + 128]
        ntri_i = cbt[:, CB_TI:CB_TI + 128]
        ntri_s = cbt[:, CB_TS:CB_TS + 128]
        ones_bf = cbt[:, CB_ON:CB_ON + 128]

        prm, prmb = sb(gs, "prm", [128, 256], F32)
        def pc(l, off, n=1):
            return prm[:, 96 * l + off: 96 * l + off + n]

        sch.dma("sp", out=prm[:, 192:200], in_=gf_d[:, :], writes=[prmb])
        sch.dma("sp", out=prm[:, 200:208], in_=c_col_d[:, :], writes=[prmb])
        for l in range(L):
            sch.dma("sp", out=pc(l, 64, 8), in_=g1_d[l], writes=[prmb])
            sch.dma("sp", out=pc(l, 72, 8), in_=g2_d[l], writes=[prmb])
            sch.dma("sp", out=prm[0:64, 96 * l + 80: 96 * l + 96], in_=gout_d[l], writes=[prmb])
            sch.dma("sp", out=prm[0:N_FOX, 216 + l: 217 + l], in_=bf_d[l], writes=[prmb])
            sch.dma("sp", out=pc(l, 0, 48), in_=bmod_d[l], writes=[prmb])

        with contextlib.ExitStack() as st:
            wm = sbn(st, "wm", [128, KC, 512], F32, 2)
            sch.op("act", lambda e: e.activation(out=prm[:, 224:232], in_=prm[:, 200:208], func=AF.Exp, scale=-1.0),
                   reads=[prmb], writes=[prmb])
            sch.op("dve", lambda e: e.tensor_scalar(out=prm[:, 224:232], in0=prm[:, 224:232], scalar1=1.0, scalar2=None, op0=ALU.add),
                   reads=[prmb], writes=[prmb])
            sch.op("dve", lambda e: e.reciprocal(out=prm[:, 232:240], in_=prm[:, 224:232]), reads=[prmb], writes=[prmb])
            sch.op("dve", lambda e: e.tensor_tensor(out=prm[:, 208:216], in0=prm[:, 200:208], in1=prm[:, 232:240], op=ALU.mult),
                   reads=[prmb], writes=[prmb])
            for l in range(L):
                sch.op("dve", lambda e: e.tensor_scalar(out=prm[0:N_FOX, 216 + l:217 + l], in0=prm[0:N_FOX, 216 + l:217 + l],
                                                        scalar1=-1.0, scalar2=None, op0=ALU.mult), reads=[prmb], writes=[prmb])
            mps, mpsb = ps[7]
            blk = 0
            for l in range(L):
                for cbk in range(12):
                    wt, wb = wm[blk % 2]
                    blk += 1
                    sch.dma("sp", out=wt[:, :, :],
                            in_=w_mod_d[l].rearrange("(k p) c -> p k c", p=128)[:, :, cbk * 512:(cbk + 1) * 512], writes=[wb])
                    for jj in range(4):
                        j = cbk * 4 + jj
                        for k in range(KC):
                            sch.op("pe", lambda e: e.matmul(mps[:, l * 48 + j: l * 48 + j + 1], lhsT=wt[:, k, jj * 128:(jj + 1) * 128],
                                                            rhs=prm[:, 208 + k: 209 + k], start=(k == 0), stop=(k == KC - 1)),
                                   reads=[wb, prmb], writes=[mpsb])
                sch.op("dve", lambda e: e.tensor_tensor(out=pc(l, 0, 48), in0=mps[:, l * 48:(l + 1) * 48], in1=pc(l, 0, 48), op=ALU.add),
                       reads=[mpsb, prmb], writes=[prmb])
                sch.op("dve", lambda e: e.scalar_tensor_tensor(out=pc(l, 48, 8), in0=pc(l, 8, 8), scalar=1.0, in1=pc(l, 64, 8),
                                                               op0=ALU.add, op1=ALU.mult), reads=[prmb], writes=[prmb])
                sch.op("dve", lambda e: e.scalar_tensor_tensor(out=pc(l, 56, 8), in0=pc(l, 32, 8), scalar=1.0, in1=pc(l, 72, 8),
                                                               op0=ALU.add, op1=ALU.mult), reads=[prmb], writes=[prmb])
            sch.barrier()
            sch.release(keep=(cbb, prmb))

        def rms_rstd(xt, xtb, sq, sqb, ssp, sspb, lnv, lnvb, rstd, rstdb):
            sch.op("act", lambda e: e.activation(out=sq[:, :, :], in_=xt[:, :, :], func=AF.Square), reads=[xtb], writes=[sqb])
            for k in range(KC):
                sch.op("pe", lambda e: e.matmul(ssp[:, :], lhsT=ones_bf, rhs=sq[:, k, :], start=(k == 0), stop=(k == KC - 1)),
                       reads=[sqb, cbb], writes=[sspb])
            sch.op("act", lambda e: e.activation(out=lnv[:, :], in_=ssp[:, :], func=AF.Ln, scale=1.0 / D, bias=EPS),
                   reads=[sspb], writes=[lnvb])
            sch.op("act", lambda e: e.activation(out=rstd[:, :], in_=lnv[:, :], func=AF.Exp, scale=-0.5), reads=[lnvb], writes=[rstdb])

        sch.op("dve", lambda e: e.memset(prm[:, 240:241], EPS), writes=[prmb])
        sch.op("dve", lambda e: e.memset(prm[:, 241:242], 1.0), writes=[prmb])
        sch.op("dve", lambda e: e.memset(prm[:, 242:243], 0.0), writes=[prmb])
        eps_col = prm[:, 240:241]
        one_col = prm[:, 241:242]
        zero_col = prm[:, 242:243]

        def xview(ap_ds):
            return ap_ds.rearrange("(k p) s -> p k s", p=128)

        def load_w_cast(dst, src_rows_ap, name, ncols, nk=KC, bw=512):
            bufs = []
            c0 = 0
            while c0 < ncols:
                c1 = min(ncols, c0 + bw)
                b = Buf("%s_c%d" % (name, c0))
                sch.dma("pool", out=dst[:, :, c0:c1], in_=src_rows_ap.rearrange("(k p) c -> p k c", p=128)[:, :, c0:c1], writes=[b])
                bufs.append(b)
                c0 = c1
            return bufs

        for l in range(L):
            xsrc = xT_in if l == 0 else xT_s
            with contextlib.ExitStack() as st:
                win, winb = sb(st, "win", [128, KC, W_IN_COLS], BF16)
                xts = sbn(st, "xt", [128, KC, T], F32, 2)
                sq, sqb = sb(st, "sq", [128, KC, T], BF16)
                tmps = sbn(st, "tmp", [128, T], F32, 2)
                lnv, lnvb = sb(st, "lnv", [128, T], F32)
                rstd, rstdb = sb(st, "rstd", [128, T], F32)
                hTs = sbn(st, "hT", [128, KC, T], BF16, 2)
                qk, qkb = sb(st, "qk", [128, 16, T], BF16)
                vsb, vsbb = sb(st, "vsb", [128, 4, D], BF16)
                fe, feb = sb(st, "fe", [N_FOX, T], F32)
                fsp, fspb = sb(st, "fsp", [N_FOX, T], F32)
                winbs = load_w_cast(win, w_in_d[l], "win", W_IN_COLS)

                def loadx(t):
                    xt, xtb = xts[t % 2]
                    sch.dma("sp", out=xt[:, :, :], in_=xview(xsrc)[:, :, t * T:(t + 1) * T], writes=[xtb])

                loadx(0)
                nev = 0
                for t in range(NT):
                    if t + 1 < NT:
                        loadx(t + 1)
                    xt, xtb = xts[t % 2]
                    hT, hTb = hTs[t % 2]
                    ssp, sspb = ps[4 + t % 2]
                    rms_rstd(xt, xtb, sq, sqb, ssp, sspb, lnv, lnvb, rstd, rstdb)
                    for k in range(KC):
                        tm, tmb = tmps[k % 2]
                        sch.op("dve", lambda e: e.tensor_tensor(out=tm[:, :], in0=xt[:, k, :], in1=rstd[:, :], op=ALU.mult),
                               reads=[xtb, rstdb], writes=[tmb])
                        sch.op("act", lambda e: e.activation(out=hT[:, k, :], in_=tm[:, :], func=AF.Identity,
                                                             bias=pc(l, 0 + k), scale=pc(l, 48 + k)),
                               reads=[tmb, prmb], writes=[hTb])
                    for c in range(16):
                        pp, ppb = ps[c % 4]
                        for k in range(KC):
                            sch.op("pe", lambda e: e.matmul(pp[:, :], lhsT=win[:, k, c * 128:(c + 1) * 128], rhs=hT[:, k, :],
                                                            start=(k == 0), stop=(k == KC - 1)), reads=[winbs[c // 4], hTb], writes=[ppb])
                        if nev % 2 == 0:
                            sch.op("act", lambda e: e.copy(out=qk[:, c, :], in_=pp[:, :]), reads=[ppb], writes=[qkb])
                        else:
                            sch.op("dve", lambda e: e.tensor_copy(out=qk[:, c, :], in_=pp[:, :]), reads=[ppb], writes=[qkb])
                        nev += 1
                    sch.dma("pool", out=qT_s.rearrange("k p s -> p k s")[:, :, t * T:(t + 1) * T], in_=qk[:, 0:8, :], reads=[qkb])
                    sch.dma("pool", out=kT_s.rearrange("k p s -> p k s")[:, :, t * T:(t + 1) * T], in_=qk[:, 8:16, :], reads=[qkb])
                    for tb in range(4):
                        for half in range(2):
                            pp, ppb = ps[nev % 4]
                            for k in range(KC):
                                sch.op("pe", lambda e: e.matmul(pp[:, :], lhsT=hT[:, k, tb * 128:(tb + 1) * 128],
                                                                rhs=win[:, k, 2 * D + half * 512: 2 * D + (half + 1) * 512],
                                                                start=(k == 0), stop=(k == KC - 1)), reads=[winbs[4 + half], hTb], writes=[ppb])
                            if nev % 2 == 0:
                                sch.op("act", lambda e: e.copy(out=vsb[:, tb, half * 512:(half + 1) * 512], in_=pp[:, :]),
                                       reads=[ppb], writes=[vsbb])
                            else:
                                sch.op("dve", lambda e: e.tensor_copy(out=vsb[:, tb, half * 512:(half + 1) * 512], in_=pp[:, :]),
                                       reads=[ppb], writes=[vsbb])
                            nev += 1
                    sch.dma("pool", out=v_s.rearrange("(n p) c -> p n c", p=128)[:, t * 4:(t + 1) * 4, :], in_=vsb[:, :, :], reads=[vsbb])
                    fp, fpb = ps[6]
                    for k in range(KC):
                        sch.op("pe", lambda e: e.matmul(fp[0:N_FOX, :], lhsT=win[:, k, 3 * D:3 * D + N_FOX], rhs=hT[:, k, :],
                                                        start=(k == 0), stop=(k == KC - 1)), reads=[winbs[6], hTb], writes=[fpb])
                    sch.op("act", lambda e: e.activation(out=fe[:, :], in_=fp[0:N_FOX, :], func=AF.Exp, scale=-1.0,
                                                         bias=prm[0:N_FOX, 216 + l:217 + l]), reads=[fpb, prmb], writes=[feb])
                    sch.op("act", lambda e: e.activation(out=fsp[:, :], in_=fe[:, :], func=AF.Ln, bias=one_col[0:N_FOX, :]),
                           reads=[feb, prmb], writes=[fspb])
                    sch.dma("pool", out=fl_s[:, t * T:(t + 1) * T], in_=fsp[:, :], reads=[fspb])
                sch.barrier()
                sch.release(keep=(cbb, prmb))

            with contextlib.ExitStack() as st:
                CH = min(S, 2048)
                onesf, onesfb = sb(st, "onesf", [N_FOX, CH], F32)
                ones3, ones3b = sb(st, "ones3", [N_FOX, 3, CH], BF16)
                fls = sbn(st, "fl", [N_FOX, CH], F32, 2)
                G, Gb = sb(st, "G", [N_FOX, CH], F32)
                G8, G8b = sb(st, "G8", [N_FOX, CH], F32)
                r1, r1b = sb(st, "r1", [N_FOX, CH], F32)
                cpos = sbn(st, "cpos", [N_FOX, 3, CH], BF16, 2)
                cneg = sbn(st, "cneg", [N_FOX, 3, CH], BF16, 2)
                carry, carryb = sb(st, "carry", [N_FOX, 2], F32)
                sch.op("dve", lambda e: e.memset(onesf[:, :], 1.0), writes=[onesfb])
                sch.op("dve", lambda e: e.memset(ones3[:, :, :], 1.0), writes=[ones3b])
                sch.op("dve", lambda e: e.memset(carry[:, :], 0.0), writes=[carryb])
                for ci in range(S // CH):
                    fl, flb = fls[ci % 2]
                    cp, cpb = cpos[ci % 2]
                    cn, cnb = cneg[ci % 2]
                    sl = slice(ci * CH, (ci + 1) * CH)
                    sch.dma("sp", out=fl[:, :], in_=fl_s[:, sl], writes=[flb])
                    sch.op("dve", lambda e: e.tensor_tensor_scan(out=G[:, :], data0=onesf[:, :], data1=fl[:, :], initial=carry[:, 0:1],
                                                                 op0=ALU.mult, op1=ALU.add), reads=[onesfb, flb, carryb], writes=[Gb])
                    sch.op("dve", lambda e: e.tensor_copy(out=carry[:, 0:1], in_=G[:, CH - 1:CH]), reads=[Gb], writes=[carryb])
                    sch.op("dve", lambda e: e.tensor_scalar(out=G8[:, :], in0=G[:, :], scalar1=8.0, scalar2=None, op0=ALU.mult),
                           reads=[Gb], writes=[G8b])
                    sch.op("dve", lambda e: e.tensor_copy(out=cp[:, 0, :], in_=G8[:, :]), reads=[G8b], writes=[cpb])
                    sch.op("dve", lambda e: e.tensor_tensor(out=r1[:, :], in0=G8[:, :], in1=cp[:, 0, :], op=ALU.subtract),
                           reads=[G8b, cpb], writes=[r1b])
                    sch.op("dve", lambda e: e.tensor_copy(out=cp[:, 1, :], in_=r1[:, :]), reads=[r1b], writes=[cpb])
                    sch.op("dve", lambda e: e.tensor_tensor(out=G8[:, :], in0=r1[:, :], in1=cp[:, 1, :], op=ALU.subtract),
                           reads=[r1b, cpb], writes=[G8b])
                    sch.op("dve", lambda e: e.tensor_copy(out=cp[:, 2, :], in_=G8[:, :]), reads=[G8b], writes=[cpb])
                    sch.op("dve", lambda e: e.tensor_scalar(out=cn[:, :, :], in0=cp[:, :, :], scalar1=-1.0, scalar2=None, op0=ALU.mult),
                           reads=[cpb], writes=[cnb])
                    sch.dma("pool", out=fak_s[:, 0:3, sl], in_=cp[:, :, :], reads=[cpb])
                    sch.dma("pool", out=fak_s[:, 3:6, sl], in_=ones3[:, :, :], reads=[ones3b])
                    sch.dma("pool", out=faq_s[:, 0:3, sl], in_=ones3[:, :, :], reads=[ones3b])
                    sch.dma("pool", out=faq_s[:, 3:6, sl], in_=cn[:, :, :], reads=[cnb])
                sch.barrier()
                sch.release(keep=(cbb, prmb))

            with contextlib.ExitStack() as st:
                qTs = sbn(st, "qT", [70, S], BF16, 2)
                kTs = sbn(st, "kT", [70, S], BF16, 2)
                vs = sbn(st, "v", [128, NB, 65], BF16, 2)
                mts = sbn(st, "mt", [128, 23 * 128], F32, 2)
                Es = sbn(st, "E", [128, T], F32, 3)
                Lps = sbn(st, "Lp", [128, T], BF16, 3)
                EGs = sbn(st, "EG", [128, T], F32, 2)
                pTs = sbn(st, "pT", [128, T], BF16, 4)
                sqs = sbn(st, "sqo", [65, T], BF16, 2)
                lns = sbn(st, "lno", [64, T], F32, 2)
                rrs = sbn(st, "rro", [64, T], F32, 2)
                osb = sbn(st, "osb", [64, T], BF16, 2)
                wss = cbt[:, CB_WS:CB_WS + 64]
                for s_ in range(2):
                    vt, vb = vs[s_]
                    sch.op("dve", lambda e: e.memset(vt[:, :, 64:65], 1.0), writes=[vb])
                    sch.op("dve", lambda e: e.memset(qTs[s_][0][64:70, :], 0.0), writes=[qTs[s_][1]])
                    sch.op("dve", lambda e: e.memset(kTs[s_][0][64:70, :], 0.0), writes=[kTs[s_][1]])

                def load_head(h):
                    s_ = h % 2
                    qt, qb = qTs[s_]
                    kt, kb = kTs[s_]
                    vt, vb = vs[s_]
                    r0 = (h % 2) * 64
                    sch.dma("sp", out=qt[0:64, :], in_=qT_s[h // 2, r0:r0 + 64, :], writes=[qb])
                    sch.dma("sp", out=kt[0:64, :], in_=kT_s[h // 2, r0:r0 + 64, :], writes=[kb])
                    if h >= N_DIL + N_FOX:
                        sch.op("dve", lambda e: e.memset(kt[64:70, :], 0.0), writes=[kb])
                    if N_DIL <= h < N_DIL + N_FOX:
                        hf = h - N_DIL
                        sch.dma("sp", out=qt[64:70, :], in_=faq_s[hf], writes=[qb])
                        sch.dma("sp", out=kt[64:70, :], in_=fak_s[hf], writes=[kb])
                    sch.dma("sp", out=vt[:, :, 0:64], in_=v_s.rearrange("(n p) c -> p n c", p=128)[:, :, h * 64:(h + 1) * 64],
                            writes=[vb])
                    if h < N_DIL:
                        mt, mb = mts[s_]
                        sch.dma("sp", out=mt[:, :], in_=mt_d[h], writes=[mb])

                epi_n = [0]

                def attn_head(h):
                    s_ = h % 2
                    kind = "dil" if h < N_DIL else ("fox" if h < N_DIL + N_FOX else "sb")
                    K = 70
                    qt, qb = qTs[s_]
                    kt, kb = kTs[s_]
                    vt, vb = vs[s_]
                    mt, mb = mts[s_]
                    pairs = []
                    for i in range(NT):
                        if kind == "dil":
                            ms = [m for m in range(4 * i - 16, 4 * i + 4) if m >= 0]
                        elif kind == "fox":
                            ms = list(range(0, 4 * i + 4))
                        else:
                            ms = list(range(4 * i + 3, -1, -1))
                        for idx, m in enumerate(ms):
                            pairs.append((i, m, idx == 0, idx == len(ms) - 1))
                    n = len(pairs)

                    def stage1(pi):
                        i, m, first, last = pairs[pi]
                        stt, stb = ps[pi % 3]
                        diag = (m >= 4 * i) and kind != "dil"
                        sch.op("pe", lambda e: e.matmul(stt[:, :], lhsT=kt[0:K, m * 128:(m + 1) * 128], rhs=qt[0:K, i * T:(i + 1) * T],
                                                        start=True, stop=not diag), reads=[kb, qb], writes=[stb])
                        if diag:
                            r = m - 4 * i
                            tab = negI if kind == "fox" else negS
                            sch.op("pe", lambda e: e.matmul(stt[:, :], lhsT=ident, rhs=tab[:, 384 - 128 * r: 896 - 128 * r],
                                                            start=False, stop=True), reads=[cbb], writes=[stb])

                    def epilogue(i):
                        ot, otb = ps[5 + i % 2]
                        mst, msb = ps[7]
                        en = epi_n[0]
                        epi_n[0] += 1
                        sqt, sqb_ = sqs[en % 2]
                        lnt, lnb = lns[en % 2]
                        rrt, rrb = rrs[en % 2]
                        ost, osb_ = osb[en % 2]
                        R = 65
                        wcol = CB_WS2 if kind == "sb" else CB_WS
                        bval = EPS if kind == "sb" else 0.0
                        sch.op("act", lambda e: e.activation(out=sqt[0:R, :], in_=ot[0:R, :], func=AF.Square), reads=[otb], writes=[sqb_])
                        sch.op("pe", lambda e: e.matmul(mst[:, :], lhsT=cbt[0:R, wcol:wcol + 128], rhs=sqt[0:R, :], start=True, stop=True),
                               reads=[sqb_, cbb], writes=[msb])
                        sch.op("act", lambda e: e.activation(out=lnt[:, :], in_=mst[0:64, :], func=AF.Ln, bias=bval),
                               reads=[msb], writes=[lnb])
                        sch.op("act", lambda e: e.activation(out=rrt[:, :], in_=lnt[:, :], func=AF.Exp, scale=-0.5), reads=[lnb], writes=[rrb])
                        sch.op("dve", lambda e: e.scalar_tensor_tensor(out=ost[:, :], in0=ot[0:64, :], scalar=prm[0:64, 96 * l + 80 + h: 96 * l + 81 + h],
                                                                       in1=rrt[:, :], op0=ALU.mult, op1=ALU.mult),
                               reads=[otb, rrb, prmb], writes=[osb_])
                        r0 = (h % 2) * 64
                        sch.dma("pool", out=oT_s[h // 2, r0:r0 + 64, i * T:(i + 1) * T], in_=ost[:, :], reads=[osb_])

                    def s2_act(pi):
                        i, m, first, last = pairs[pi]
                        stt, stb = ps[pi % 3]
                        pt, pb = pTs[pi % 4]
                        if kind == "fox":
                            sch.op("act", lambda e: e.activation(out=pt[:, :], in_=stt[:, :], func=AF.Exp, scale=0.125), reads=[stb], writes=[pb])
                        elif kind == "dil":
                            Et, Eb = Es[pi % 3]
                            d0 = 4 * i - m
                            sch.op("act", lambda e: e.activation(out=Et[:, :], in_=stt[:, :], func=AF.Exp, scale=0.125), reads=[stb], writes=[Eb])
                            sch.op("dve", lambda e: e.tensor_tensor(out=pt[:, :], in0=Et[:, :], in1=mt[:, (d0 + 3) * 128:(d0 + 7) * 128], op=ALU.mult),
                                   reads=[Eb, mb], writes=[pb])
                        else:
                            Et, Eb = Es[pi % 3]
                            Lt, Lb = Lps[pi % 3]
                            sch.op("act", lambda e: e.activation(out=Et[:, :], in_=stt[:, :], func=AF.Exp, scale=0.125), reads=[stb], writes=[Eb])
                            sch.op("act", lambda e: e.activation(out=Lt[:, :], in_=Et[:, :], func=AF.Ln, bias=1.0), reads=[Eb], writes=[Lb])

                    def s3_tri1(pi):
                        i, m, first, last = pairs[pi]
                        Lt, Lb = Lps[pi % 3]
                        act_, accb = ps[3 + i % 2]
                        sch.op("pe", lambda e: e.matmul(act_[:, :], lhsT=ntri_i, rhs=Lt[:, :], start=first, stop=True, skip_group_check=not first),
                               reads=[Lb, cbb], writes=[accb])

                    def s3_eg(pi):
                        i, m, first, last = pairs[pi]
                        pt, pb = pTs[pi % 4]
                        Et, Eb = Es[pi % 3]
                        Gt, Gb_ = EGs[pi % 2]
                        act_, accb = ps[3 + i % 2]
                        sch.op("act", lambda e: e.activation(out=Gt[:, :], in_=act_[:, :], func=AF.Exp), reads=[accb], writes=[Gb_])
                        sch.op("dve", lambda e: e.tensor_tensor(out=pt[:, :], in0=Et[:, :], in1=Gt[:, :], op=ALU.mult),
                               reads=[Eb, Gb_], writes=[pb])

                    def s3_tri2(pi):
                        i, m, first, last = pairs[pi]
                        if last:
                            return
                        Lt, Lb = Lps[pi % 3]
                        act_, accb = ps[3 + i % 2]
                        sch.op("pe", lambda e: e.matmul(act_[:, :], lhsT=ntri_s, rhs=Lt[:, :], start=False, stop=True, skip_group_check=True),
                               reads=[Lb, cbb], writes=[accb])

                    def s4_pv(pi):
                        i, m, first, last = pairs[pi]
                        ot, otb = ps[5 + i % 2]
                        pt, pb = pTs[pi % 4]
                        R = 65
                        sch.op("pe", lambda e: e.matmul(ot[0:R, :], lhsT=vt[:, m, 0:R], rhs=pt[:, :], start=first, stop=last),
                               reads=[vb, pb], writes=[otb])
                        if last:
                            epilogue(i)

                    if kind == "sb":
                        stages = [(s3_tri2, 3), (s3_tri1, 2), (stage1, 0), (s2_act, 1), (s3_eg, 2), (s4_pv, 3)]
                    else:
                        stages = [(stage1, 0), (s2_act, 2), (s4_pv, 3)]
                    depth = max(lag for _, lag in stages)
                    for idx in range(n + depth):
                        for fn, lag in stages:
                            p = idx - lag
                            if 0 <= p < n:
                                fn(p)

                load_head(0)
                for h in range(NH):
                    if h + 1 < NH:
                        load_head(h + 1)
                    attn_head(h)
                sch.barrier()
                sch.release(keep=(cbb, prmb))

            with contextlib.ExitStack() as st:
                wo, wob = sb(st, "wo", [128, KC, D], BF16)
                xts = sbn(st, "xt", [128, KC, T], F32, 2)
                oTs = sbn(st, "oTt", [128, KC, T], BF16, 2)
                sq, sqb = sb(st, "sq", [128, KC, T], BF16)
                tmps = sbn(st, "tmp", [128, T], F32, 2)
                lnv, lnvb = sb(st, "lnv", [128, T], F32)
                rstd, rstdb = sb(st, "rstd", [128, T], F32)
                hTs = sbn(st, "hT", [128, KC, T], BF16, 2)
                wobs = load_w_cast(wo, w_out_d[l], "wo", D)

                def loadc1(t):
                    xt, xtb = xts[t % 2]
                    ott, otb = oTs[t % 2]
                    sch.dma("sp", out=xt[:, :, :], in_=xview(xsrc)[:, :, t * T:(t + 1) * T], writes=[xtb])
                    sch.dma("sp", out=ott[:, :, :], in_=oT_s.rearrange("k p s -> p k s")[:, :, t * T:(t + 1) * T], writes=[otb])

                loadc1(0)
                for t in range(NT):
                    if t + 1 < NT:
                        loadc1(t + 1)
                    xt, xtb = xts[t % 2]
                    ott, otb = oTs[t % 2]
                    hT, hTb = hTs[t % 2]
                    for c in range(KC):
                        pp, ppb = ps[c % 4]
                        for k in range(KC):
                            sch.op("pe", lambda e: e.matmul(pp[:, :], lhsT=wo[:, k, c * 128:(c + 1) * 128], rhs=ott[:, k, :],
                                                            start=(k == 0), stop=(k == KC - 1)), reads=[wobs[c // 4], otb], writes=[ppb])
                        sch.op("dve", lambda e: e.scalar_tensor_tensor(out=xt[:, c, :], in0=pp[:, :], scalar=pc(l, 16 + c), in1=xt[:, c, :],
                                                                       op0=ALU.mult, op1=ALU.add), reads=[ppb, xtb, prmb], writes=[xtb])
                    ssp, sspb = ps[4 + t % 2]
                    rms_rstd(xt, xtb, sq, sqb, ssp, sspb, lnv, lnvb, rstd, rstdb)
                    for k in range(KC):
                        tm, tmb = tmps[k % 2]
                        sch.op("dve", lambda e: e.tensor_tensor(out=tm[:, :], in0=xt[:, k, :], in1=rstd[:, :], op=ALU.mult),
                               reads=[xtb, rstdb], writes=[tmb])
                        sch.op("act", lambda e: e.activation(out=hT[:, k, :], in_=tm[:, :], func=AF.Identity,
                                                             bias=pc(l, 24 + k), scale=pc(l, 56 + k)),
                               reads=[tmb, prmb], writes=[hTb])
                    sch.dma("pool", out=xview(xT_s)[:, :, t * T:(t + 1) * T], in_=xt[:, :, :], reads=[xtb])
                    sch.dma("pool", out=h2T_s.rearrange("k p s -> p k s")[:, :, t * T:(t + 1) * T], in_=hT[:, :, :], reads=[hTb])
                sch.barrier()
                sch.release(keep=(cbb, prmb))

            for half in range(2):
                last_phase = (l == L - 1 and half == 1)
                with contextlib.ExitStack() as st:
                    HH = DFF // 2
                    w1t, w1b = sb(st, "w1t", [128, KC, HH], BF16)
                    w2t, w2b = sb(st, "w2t", [128, HH // 128, D], BF16)
                    xts = sbn(st, "xt", [128, KC, T], F32, 2)
                    hTs = sbn(st, "hT", [128, KC, T], BF16, 2)
                    rbs = sbn(st, "rb", [128, T], F32, 2)
                    hid, hidb = sb(st, "hid", [128, HH // 128, T], BF16)
                    if last_phase:
                        sq, sqb = sb(st, "sq", [128, KC, T], BF16)
                        lnv, lnvb = sb(st, "lnv", [128, T], F32)
                        rstd, rstdb = sb(st, "rstd", [128, T], F32)
                        outs = sbn(st, "ot", [128, KC, T], F32, 1)
                    w1bs = load_w_cast(w1t, w1_d[l][:, half * HH:(half + 1) * HH], "w1", HH)
                    w2bs = load_w_cast(w2t, w2_d[l][half * HH:(half + 1) * HH, :], "w2", D, nk=HH // 128, bw=256)

                    def loadc2(t):
                        xt, xtb = xts[t % 2]
                        hT, hTb = hTs[t % 2]
                        sch.dma("sp", out=xt[:, :, :], in_=xview(xT_s)[:, :, t * T:(t + 1) * T], writes=[xtb])
                        sch.dma("sp", out=hT[:, :, :], in_=h2T_s.rearrange("k p s -> p k s")[:, :, t * T:(t + 1) * T], writes=[hTb])

                    loadc2(0)
                    for t in range(NT):
                        if t + 1 < NT:
                            loadc2(t + 1)
                        xt, xtb = xts[t % 2]
                        hT, hTb = hTs[t % 2]
                        for j in range(HH // 128):
                            pp, ppb = ps[j % 4]
                            rb, rbb = rbs[j % 2]
                            for k in range(KC):
                                sch.op("pe", lambda e: e.matmul(pp[:, :], lhsT=w1t[:, k, j * 128:(j + 1) * 128], rhs=hT[:, k, :],
                                                                start=(k == 0), stop=(k == KC - 1)), reads=[w1bs[j // 4], hTb], writes=[ppb])
                            sch.op("act", lambda e: e.activation(out=rb[:, :], in_=pp[:, :], func=AF.Relu), reads=[ppb], writes=[rbb])
                            sch.op("dve", lambda e: e.scalar_tensor_tensor(out=hid[:, j, :], in0=pp[:, :], scalar=0.0, in1=rb[:, :],
                                                                           op0=ALU.max, op1=ALU.mult), reads=[ppb, rbb], writes=[hidb])
                        for c in range(KC):
                            pp, ppb = ps[4 + c % 4]
                            nj = HH // 128
                            for j in range(nj):
                                sch.op("pe", lambda e: e.matmul(pp[:, :], lhsT=w2t[:, j, c * 128:(c + 1) * 128], rhs=hid[:, j, :],
                                                                start=(j == 0), stop=(j == nj - 1)), reads=[w2bs[c // 2], hidb], writes=[ppb])
                            sch.op("dve", lambda e: e.scalar_tensor_tensor(out=xt[:, c, :], in0=pp[:, :], scalar=pc(l, 40 + c), in1=xt[:, c, :],
                                                                           op0=ALU.mult, op1=ALU.add), reads=[ppb, xtb, prmb], writes=[xtb])
                        if not last_phase:
                            sch.dma("pool", out=xview(xT_s)[:, :, t * T:(t + 1) * T], in_=xt[:, :, :], reads=[xtb])
                        else:
                            ot, otb = outs[0]
                            ssp, sspb = ps[0]
                            rms_rstd(xt, xtb, sq, sqb, ssp, sspb, lnv, lnvb, rstd, rstdb)
                            for k in range(KC):
                                sch.op("dve", lambda e: e.scalar_tensor_tensor(out=ot[:, k, :], in0=xt[:, k, :], scalar=prm[:, 192 + k:193 + k],
                                                                               in1=rstd[:, :], op0=ALU.mult, op1=ALU.mult),
                                       reads=[xtb, rstdb, prmb], writes=[otb])
                            sch.dma("pool", out=xview(outT)[:, :, t * T:(t + 1) * T], in_=ot[:, :, :], reads=[otb])
                    sch.barrier()
                    sch.release(keep=(cbb, prmb))
                sch.release(keep=(cbb, prmb))
        sch.barrier()
        print("instructions emitted:", sch.ninst)
    return nc


_CACHE = {}


def kernel(x, c, w_mod, b_mod, g_norm1, w_in, b_f, g_out, w_out, g_norm2, w_mlp_in, w_mlp_out, g_final):
    x = np.asarray(x, np.float32)
    B, S, _ = x.shape
    L = int(np.asarray(w_mod).shape[0])
    assert B == N_CORES
    key = (S, L)
    if key not in _CACHE:
        _CACHE[key] = build(S, L)
    nc = _CACHE[key]
    mt, cb = _const_tables()
    f = lambda a: np.ascontiguousarray(np.asarray(a, np.float32))
    col = lambda a: np.ascontiguousarray(np.asarray(a, np.float32).reshape(-1, 128).T)
    shared = {
        "w_mod": f(w_mod),
        "bmod_col": np.stack([col(np.asarray(b_mod)[l]) for l in range(L)]),
        "g1_col": np.stack([col(np.asarray(g_norm1)[l]) for l in range(L)]),
        "g2_col": np.stack([col(np.asarray(g_norm2)[l]) for l in range(L)]),
        "gout_col": np.stack([np.ascontiguousarray(np.asarray(g_out, np.float32)[l].reshape(NH, HD).T) for l in range(L)]),
        "gf_col": col(g_final),
        "bf_col": f(b_f).reshape(L, N_FOX, 1),
        "w_in": f(w_in), "w_out": f(w_out), "w1": f(w_mlp_in), "w2": f(w_mlp_out),
        "mtab": mt, "cb": cb,
    }
    c = np.asarray(c, np.float32)
    in_maps = []
    for b in range(B):
        m = dict(shared)
        m["xT"] = np.ascontiguousarray(x[b].T)
        m["c_col"] = col(c[b])
        in_maps.append(m)
    res = run_bass_kernel_spmd(nc, in_maps, core_ids=list(range(N_CORES)))
    out = np.stack([np.ascontiguousarray(res.results[b]["outT"].T) for b in range(B)], axis=0)
    return out.astype(np.float32)
```

```python
import contextlib
import numpy as np
import concourse.bass as bass
import concourse.mybir as mybir
from concourse.bass_utils import run_bass_kernel_spmd

F32 = mybir.dt.float32
BF16 = mybir.dt.bfloat16
AF = mybir.ActivationFunctionType
ALU = mybir.AluOpType

D = 1024
KC = 8
T = 512
NH = 16
HD = 64
N_DIL, N_FOX, N_SB = 6, 5, 5
DFF = 4096
W_IN_COLS = 3 * D + N_FOX
EPS = 1e-6
NEG = -1.0e9
N_CORES = 8


class Buf:
    __slots__ = ("name", "last_write", "readers", "ds")

    def __init__(self, name):
        self.name = name
        self.last_write = None
        self.readers = []
        self.ds = {}


class Sched:
    def __init__(self, nc, stack):
        self.nc = nc
        self.stack = stack
        self.E = {"pe": nc.tensor, "act": nc.scalar, "dve": nc.vector, "pool": nc.gpsimd, "sp": nc.sync}
        self.sem = {}
        self.cnt = {}
        for e in ("pe", "act", "dve", "pool"):
            self.sem[e] = stack.enter_context(nc.semaphore("s_" + e))
            self.cnt[e] = 0
        self.seen = {e: {} for e in self.E}
        self.ninst = 0
        self.dma_bufs = []
        self.free_sems = {"hw": [], "sw": []}
        self.nsem = 0

    def _wait(self, e, rec):
        _, sem, val = rec
        d = self.seen[e]
        k = id(sem)
        if d.get(k, 0) >= val:
            return
        d[k] = val
        self.E[e].wait_ge(sem, val)
        self.ninst += 1

    @staticmethod
    def _compact(readers):
        best = {}
        for r in readers:
            k = id(r[1])
            if k not in best or best[k][2] < r[2]:
                best[k] = r
        return list(best.values())

    def _finish(self, rec, reads, writes):
        for b in reads:
            b.readers.append(rec)
            if len(b.readers) > 6:
                b.readers = self._compact(b.readers)
        for b in writes:
            b.last_write = rec
            b.readers = []

    def _wait_all(self, e, recs):
        best = {}
        for r in recs:
            k = id(r[1])
            if k not in best or best[k][2] < r[2]:
                best[k] = r
        for r in best.values():
            self._wait(e, r)

    def op(self, e, fn, reads=(), writes=()):
        recs = []
        for b in reads:
            if b.last_write is not None:
                recs.append(b.last_write)
        for b in writes:
            w = b.last_write
            if w is not None and w[0] != e:
                recs.append(w)
            for r in b.readers:
                if r[0] == e:
                    continue
                recs.append(r)
        self._wait_all(e, recs)
        ins = fn(self.E[e])
        self.cnt[e] += 1
        ins.then_inc(self.sem[e], 1)
        self.ninst += 1
        self._finish((e, self.sem[e], self.cnt[e]), reads, writes)
        return ins

    def dma(self, q, out, in_, reads=(), writes=(), owner=None, **kw):
        recs = []
        for b in reads:
            if b.last_write is not None:
                recs.append(b.last_write)
        for b in writes:
            if b.last_write is not None:
                recs.append(b.last_write)
            recs.extend(b.readers)
        self._wait_all(q, recs)
        if owner is None:
            owner = (list(writes) + list(reads))[0]
        qc = "sw" if q == "pool" else "hw"
        d = owner.ds.get(qc)
        if d is None:
            if self.free_sems[qc]:
                d = self.free_sems[qc].pop()
            else:
                d = [self.stack.enter_context(self.nc.semaphore("d%s%d" % (qc, self.nsem))), 0]
                self.nsem += 1
            owner.ds[qc] = d
            if owner not in self.dma_bufs:
                self.dma_bufs.append(owner)
        d[1] += 16
        ins = self.E[q].dma_start(out=out, in_=in_, **kw)
        ins.then_inc(d[0], 16)
        self.ninst += 1
        self._finish(("dma", d[0], d[1]), reads, writes)
        return ins

    def barrier(self):
        for e in self.E:
            for e2 in ("pe", "act", "dve", "pool"):
                if self.cnt[e2] > 0 and e2 != e:
                    self._wait(e, (e2, self.sem[e2], self.cnt[e2]))
            for b in self.dma_bufs:
                for d in b.ds.values():
                    if d[1] > 0:
                        self._wait(e, ("dma", d[0], d[1]))

    def release(self, keep=()):
        kept = []
        for b in self.dma_bufs:
            if b in keep:
                kept.append(b)
            else:
                for qc, d in b.ds.items():
                    self.free_sems[qc].append(d)
                b.ds = {}
        self.dma_bufs = kept


def _const_tables():
    j = np.arange(128)[:, None]
    slopes = 2.0 ** (-8.0 * np.arange(1, N_DIL + 1, dtype=np.float64) / N_DIL)
    d = np.arange(23)[None, :, None] - 3
    i = np.arange(128)[None, None, :]
    delta = 128 * d + i - j[:, :, None]
    cnt = ((delta >= 0) & (delta <= 128)).astype(np.float64)
    cnt += ((delta >= 0) & (delta <= 512) & (delta % 4 == 0))
    cnt += ((delta >= 0) & (delta <= 2048) & (delta % 16 == 0))
    mt = np.zeros((N_DIL, 128, 23, 128), np.float64)
    for h in range(N_DIL):
        mt[h] = cnt * np.exp(-slopes[h] * np.maximum(delta, 0))
    mt = mt.reshape(N_DIL, 128, 23 * 128).astype(np.float32)
    u = np.arange(896)[None, :]
    neg_incl = np.where(j > u - 384, NEG, 0.0).astype(np.float32)
    neg_strict = np.where(j >= u - 384, NEG, 0.0).astype(np.float32)
    s = np.arange(128)[None, :]
    ident = (j == s).astype(np.float32)
    ntri_incl = np.where(j >= s, -1.0, 0.0).astype(np.float32)
    ntri_strc = np.where(j < s, -1.0, 0.0).astype(np.float32)
    ones = np.ones((128, 128), np.float32)
    wss = np.zeros((128, 64), np.float32)
    wss[0:64] = 1.0 / 64.0
    wss[64] = EPS
    wss2 = wss.copy()
    wss2[64] = 0.0
    cb = np.concatenate([neg_incl, neg_strict, ident, ntri_incl, ntri_strc, ones, np.pad(wss, ((0, 0), (0, 64))),
                         np.pad(wss2, ((0, 0), (0, 64)))], axis=1)
    return mt, np.ascontiguousarray(cb)


CB_NI, CB_NS, CB_ID, CB_TI, CB_TS, CB_ON, CB_WS = 0, 896, 1792, 1920, 2048, 2176, 2304
CB_WS2 = 2432
CB_W = 2560


def build(S, L):
    NT = S // T
    NB = S // 128
    nc = bass.Bass("TRN2", target_bir_lowering=False)

    def dram(name, shape, dt, kind):
        return nc.dram_tensor(name, shape, dt, kind=kind).ap()

    xT_in = dram("xT", [D, S], F32, "ExternalInput")
    c_col_d = dram("c_col", [128, KC], F32, "ExternalInput")
    w_mod_d = dram("w_mod", [L, D, 6 * D], F32, "ExternalInput")
    bmod_d = dram("bmod_col", [L, 128, 48], F32, "ExternalInput")
    g1_d = dram("g1_col", [L, 128, KC], F32, "ExternalInput")
    g2_d = dram("g2_col", [L, 128, KC], F32, "ExternalInput")
    gout_d = dram("gout_col", [L, 64, NH], F32, "ExternalInput")
    gf_d = dram("gf_col", [128, KC], F32, "ExternalInput")
    bf_d = dram("bf_col", [L, N_FOX, 1], F32, "ExternalInput")
    w_in_d = dram("w_in", [L, D, W_IN_COLS], F32, "ExternalInput")
    w_out_d = dram("w_out", [L, D, D], F32, "ExternalInput")
    w1_d = dram("w1", [L, D, DFF], F32, "ExternalInput")
    w2_d = dram("w2", [L, DFF, D], F32, "ExternalInput")
    mt_d = dram("mtab", [N_DIL, 128, 23 * 128], F32, "ExternalInput")
    cb_d = dram("cb", [128, CB_W], F32, "ExternalInput")
    outT = dram("outT", [D, S], F32, "ExternalOutput")

    xT_s = dram("xT_s", [D, S], F32, "Internal")
    qT_s = dram("qT_s", [KC, 128, S], BF16, "Internal")
    kT_s = dram("kT_s", [KC, 128, S], BF16, "Internal")
    v_s = dram("v_s", [S, D], BF16, "Internal")
    oT_s = dram("oT_s", [KC, 128, S], BF16, "Internal")
    h2T_s = dram("h2T_s", [KC, 128, S], BF16, "Internal")
    fl_s = dram("fl_s", [N_FOX, S], F32, "Internal")
    faq_s = dram("faq_s", [N_FOX, 6, S], BF16, "Internal")
    fak_s = dram("fak_s", [N_FOX, 6, S], BF16, "Internal")

    with contextlib.ExitStack() as gs:
        sch = Sched(nc, gs)

        uniq = [0]

        def sb(stack, name, shape, dt):
            uniq[0] += 1
            name = "%s_%d" % (name, uniq[0])
            t = stack.enter_context(nc.sbuf_tensor(name, shape, dt))
            return t, Buf(name)

        def sbn(stack, name, shape, dt, n):
            return [sb(stack, "%s%d" % (name, i), shape, dt) for i in range(n)]

        ps = []
        for i in range(8):
            t = gs.enter_context(nc.psum_tensor("ps%d" % i, [128, 512], F32))
            ps.append((t, Buf("ps%d" % i)))

        cbt, cbb = sb(gs, "cbt", [128, CB_W], BF16)
        sch.dma("pool", out=cbt[:, :], in_=cb_d[:, :], writes=[cbb])
        negI = cbt[:, CB_NI:CB_NI + 896]
        negS = cbt[:, CB_NS:CB_NS + 896]
        ident = cbt[:, CB_ID:CB_ID + 128]
        ntri_i = cbt[:, CB_TI:CB_TI + 128]
        ntri_s = cbt[:, CB_TS:CB_TS + 128]
        ones_bf = cbt[:, CB_ON:CB_ON + 128]

        prm, prmb = sb(gs, "prm", [128, 256], F32)
        def pc(l, off, n=1):
            return prm[:, 96 * l + off: 96 * l + off + n]

        sch.dma("sp", out=prm[:, 192:200], in_=gf_d[:, :], writes=[prmb])
        sch.dma("sp", out=prm[:, 200:208], in_=c_col_d[:, :], writes=[prmb])
        for l in range(L):
            sch.dma("sp", out=pc(l, 64, 8), in_=g1_d[l], writes=[prmb])
            sch.dma("sp", out=pc(l, 72, 8), in_=g2_d[l], writes=[prmb])
            sch.dma("sp", out=prm[0:64, 96 * l + 80: 96 * l + 96], in_=gout_d[l], writes=[prmb])
            sch.dma("sp", out=prm[0:N_FOX, 216 + l: 217 + l], in_=bf_d[l], writes=[prmb])
            sch.dma("sp", out=pc(l, 0, 48), in_=bmod_d[l], writes=[prmb])

        with contextlib.ExitStack() as st:
            wm = sbn(st, "wm", [128, KC, 512], F32, 2)
            sch.op("act", lambda e: e.activation(out=prm[:, 224:232], in_=prm[:, 200:208], func=AF.Exp, scale=-1.0),
                   reads=[prmb], writes=[prmb])
            sch.op("dve", lambda e: e.tensor_scalar(out=prm[:, 224:232], in0=prm[:, 224:232], scalar1=1.0, scalar2=None, op0=ALU.add),
                   reads=[prmb], writes=[prmb])
            sch.op("dve", lambda e: e.reciprocal(out=prm[:, 232:240], in_=prm[:, 224:232]), reads=[prmb], writes=[prmb])
            sch.op("dve", lambda e: e.tensor_tensor(out=prm[:, 208:216], in0=prm[:, 200:208], in1=prm[:, 232:240], op=ALU.mult),
                   reads=[prmb], writes=[prmb])
            for l in range(L):
                sch.op("dve", lambda e: e.tensor_scalar(out=prm[0:N_FOX, 216 + l:217 + l], in0=prm[0:N_FOX, 216 + l:217 + l],
                                                        scalar1=-1.0, scalar2=None, op0=ALU.mult), reads=[prmb], writes=[prmb])
            mps, mpsb = ps[7]
            blk = 0
            for l in range(L):
                for cbk in range(12):
                    wt, wb = wm[blk % 2]
                    blk += 1
                    sch.dma("sp", out=wt[:, :, :],
                            in_=w_mod_d[l].rearrange("(k p) c -> p k c", p=128)[:, :, cbk * 512:(cbk + 1) * 512], writes=[wb])
                    for jj in range(4):
                        j = cbk * 4 + jj
                        for k in range(KC):
                            sch.op("pe", lambda e: e.matmul(mps[:, l * 48 + j: l * 48 + j + 1], lhsT=wt[:, k, jj * 128:(jj + 1) * 128],
                                                            rhs=prm[:, 208 + k: 209 + k], start=(k == 0), stop=(k == KC - 1)),
                                   reads=[wb, prmb], writes=[mpsb])
                sch.op("dve", lambda e: e.tensor_tensor(out=pc(l, 0, 48), in0=mps[:, l * 48:(l + 1) * 48], in1=pc(l, 0, 48), op=ALU.add),
                       reads=[mpsb, prmb], writes=[prmb])
                sch.op("dve", lambda e: e.scalar_tensor_tensor(out=pc(l, 48, 8), in0=pc(l, 8, 8), scalar=1.0, in1=pc(l, 64, 8),
                                                               op0=ALU.add, op1=ALU.mult), reads=[prmb], writes=[prmb])
                sch.op("dve", lambda e: e.scalar_tensor_tensor(out=pc(l, 56, 8), in0=pc(l, 32, 8), scalar=1.0, in1=pc(l, 72, 8),
                                                               op0=ALU.add, op1=ALU.mult), reads=[prmb], writes=[prmb])
            sch.barrier()
            sch.release(keep=(cbb, prmb))

        def rms_rstd(xt, xtb, sq, sqb, ssp, sspb, lnv, lnvb, rstd, rstdb):
            sch.op("act", lambda e: e.activation(out=sq[:, :, :], in_=xt[:, :, :], func=AF.Square), reads=[xtb], writes=[sqb])
            for k in range(KC):
                sch.op("pe", lambda e: e.matmul(ssp[:, :], lhsT=ones_bf, rhs=sq[:, k, :], start=(k == 0), stop=(k == KC - 1)),
                       reads=[sqb, cbb], writes=[sspb])
            sch.op("act", lambda e: e.activation(out=lnv[:, :], in_=ssp[:, :], func=AF.Ln, scale=1.0 / D, bias=EPS),
                   reads=[sspb], writes=[lnvb])
            sch.op("act", lambda e: e.activation(out=rstd[:, :], in_=lnv[:, :], func=AF.Exp, scale=-0.5), reads=[lnvb], writes=[rstdb])

        def norm_gen(l, xt, xtb, hT, hTb, ssp, sspb, sq, sqb, lnv, lnvb, rstd, rstdb, tmps, boff, soff):
            sch.op("act", lambda e: e.activation(out=sq[:, :, :], in_=xt[:, :, :], func=AF.Square), reads=[xtb], writes=[sqb])
            yield
            for k in range(KC):
                sch.op("pe", lambda e: e.matmul(ssp[:, :], lhsT=ones_bf, rhs=sq[:, k, :], start=(k == 0), stop=(k == KC - 1)),
                       reads=[sqb, cbb], writes=[sspb])
            yield
            sch.op("act", lambda e: e.activation(out=lnv[:, :], in_=ssp[:, :], func=AF.Ln, scale=1.0 / D, bias=EPS),
                   reads=[sspb], writes=[lnvb])
            sch.op("act", lambda e: e.activation(out=rstd[:, :], in_=lnv[:, :], func=AF.Exp, scale=-0.5), reads=[lnvb], writes=[rstdb])
            yield
            for k in range(KC):
                tm, tmb = tmps[k % 2]
                sch.op("dve", lambda e: e.tensor_tensor(out=tm[:, :], in0=xt[:, k, :], in1=rstd[:, :], op=ALU.mult),
                       reads=[xtb, rstdb], writes=[tmb])
                sch.op("act", lambda e: e.activation(out=hT[:, k, :], in_=tm[:, :], func=AF.Identity,
                                                     bias=pc(l, boff + k), scale=pc(l, soff + k)),
                       reads=[tmb, prmb], writes=[hTb])
                yield

        sch.op("dve", lambda e: e.memset(prm[:, 240:241], EPS), writes=[prmb])
        sch.op("dve", lambda e: e.memset(prm[:, 241:242], 1.0), writes=[prmb])
        sch.op("dve", lambda e: e.memset(prm[:, 242:243], 0.0), writes=[prmb])
        eps_col = prm[:, 240:241]
        one_col = prm[:, 241:242]
        zero_col = prm[:, 242:243]

        def xview(ap_ds):
            return ap_ds.rearrange("(k p) s -> p k s", p=128)

        def load_w_cast(dst, src_rows_ap, name, ncols, nk=KC, bw=512):
            bufs = []
            c0 = 0
            while c0 < ncols:
                c1 = min(ncols, c0 + bw)
                b = Buf("%s_c%d" % (name, c0))
                sch.dma("pool", out=dst[:, :, c0:c1], in_=src_rows_ap.rearrange("(k p) c -> p k c", p=128)[:, :, c0:c1], writes=[b])
                bufs.append(b)
                c0 = c1
            return bufs

        for l in range(L):
            xsrc = xT_in if l == 0 else xT_s
            with contextlib.ExitStack() as st:
                win, winb = sb(st, "win", [128, KC, W_IN_COLS], BF16)
                xts = sbn(st, "xt", [128, KC, T], F32, 2)
                sq, sqb = sb(st, "sq", [128, KC, T], BF16)
                tmps = sbn(st, "tmp", [128, T], F32, 2)
                lnv, lnvb = sb(st, "lnv", [128, T], F32)
                rstd, rstdb = sb(st, "rstd", [128, T], F32)
                hTs = sbn(st, "hT", [128, KC, T], BF16, 2)
                qk, qkb = sb(st, "qk", [128, 16, T], BF16)
                vsb, vsbb = sb(st, "vsb", [128, 4, D], BF16)
                fe, feb = sb(st, "fe", [N_FOX, T], F32)
                fsp, fspb = sb(st, "fsp", [N_FOX, T], F32)
                winbs = load_w_cast(win, w_in_d[l], "win", W_IN_COLS)

                def loadx(t):
                    xt, xtb = xts[t % 2]
                    sch.dma("sp", out=xt[:, :, :], in_=xview(xsrc)[:, :, t * T:(t + 1) * T], writes=[xtb])

                def normA(t):
                    return norm_gen(l, xts[t % 2][0], xts[t % 2][1], hTs[t % 2][0], hTs[t % 2][1], ps[4 + t % 2][0], ps[4 + t % 2][1],
                                    sq, sqb, lnv, lnvb, rstd, rstdb, tmps, 0, 48)

                loadx(0)
                for _ in normA(0):
                    pass
                nev = 0
                for t in range(NT):
                    if t + 1 < NT:
                        loadx(t + 1)
                        ngen = normA(t + 1)
                    else:
                        ngen = iter(())
                    xt, xtb = xts[t % 2]
                    hT, hTb = hTs[t % 2]
                    for c in range(16):
                        pp, ppb = ps[c % 4]
                        for k in range(KC):
                            sch.op("pe", lambda e: e.matmul(pp[:, :], lhsT=win[:, k, c * 128:(c + 1) * 128], rhs=hT[:, k, :],
                                                            start=(k == 0), stop=(k == KC - 1)), reads=[winbs[c // 4], hTb], writes=[ppb])
                        if nev % 2 == 0:
                            sch.op("act", lambda e: e.copy(out=qk[:, c, :], in_=pp[:, :]), reads=[ppb], writes=[qkb])
                        else:
                            sch.op("dve", lambda e: e.tensor_copy(out=qk[:, c, :], in_=pp[:, :]), reads=[ppb], writes=[qkb])
                        nev += 1
                        next(ngen, None)
                    sch.dma("pool", out=qT_s.rearrange("k p s -> p k s")[:, :, t * T:(t + 1) * T], in_=qk[:, 0:8, :], reads=[qkb])
                    sch.dma("pool", out=kT_s.rearrange("k p s -> p k s")[:, :, t * T:(t + 1) * T], in_=qk[:, 8:16, :], reads=[qkb])
                    for tb in range(4):
                        for half in range(2):
                            pp, ppb = ps[nev % 4]
                            for k in range(KC):
                                sch.op("pe", lambda e: e.matmul(pp[:, :], lhsT=hT[:, k, tb * 128:(tb + 1) * 128],
                                                                rhs=win[:, k, 2 * D + half * 512: 2 * D + (half + 1) * 512],
                                                                start=(k == 0), stop=(k == KC - 1)), reads=[winbs[4 + half], hTb], writes=[ppb])
                            if nev % 2 == 0:
                                sch.op("act", lambda e: e.copy(out=vsb[:, tb, half * 512:(half + 1) * 512], in_=pp[:, :]),
                                       reads=[ppb], writes=[vsbb])
                            else:
                                sch.op("dve", lambda e: e.tensor_copy(out=vsb[:, tb, half * 512:(half + 1) * 512], in_=pp[:, :]),
                                       reads=[ppb], writes=[vsbb])
                            nev += 1
                            next(ngen, None)
                    sch.dma("pool", out=v_s.rearrange("(n p) c -> p n c", p=128)[:, t * 4:(t + 1) * 4, :], in_=vsb[:, :, :], reads=[vsbb])
                    for _ in ngen:
                        pass
                    fp, fpb = ps[6]
                    for k in range(KC):
                        sch.op("pe", lambda e: e.matmul(fp[0:N_FOX, :], lhsT=win[:, k, 3 * D:3 * D + N_FOX], rhs=hT[:, k, :],
                                                        start=(k == 0), stop=(k == KC - 1)), reads=[winbs[6], hTb], writes=[fpb])
                    sch.op("act", lambda e: e.activation(out=fe[:, :], in_=fp[0:N_FOX, :], func=AF.Exp, scale=-1.0,
                                                         bias=prm[0:N_FOX, 216 + l:217 + l]), reads=[fpb, prmb], writes=[feb])
                    sch.op("act", lambda e: e.activation(out=fsp[:, :], in_=fe[:, :], func=AF.Ln, bias=one_col[0:N_FOX, :]),
                           reads=[feb, prmb], writes=[fspb])
                    sch.dma("pool", out=fl_s[:, t * T:(t + 1) * T], in_=fsp[:, :], reads=[fspb])
                sch.barrier()
                sch.release(keep=(cbb, prmb))

            with contextlib.ExitStack() as st:
                CH = min(S, 2048)
                onesf, onesfb = sb(st, "onesf", [N_FOX, CH], F32)
                ones3, ones3b = sb(st, "ones3", [N_FOX, 3, CH], BF16)
                fls = sbn(st, "fl", [N_FOX, CH], F32, 2)
                G, Gb = sb(st, "G", [N_FOX, CH], F32)
                G8, G8b = sb(st, "G8", [N_FOX, CH], F32)
                r1, r1b = sb(st, "r1", [N_FOX, CH], F32)
                cpos = sbn(st, "cpos", [N_FOX, 3, CH], BF16, 2)
                cneg = sbn(st, "cneg", [N_FOX, 3, CH], BF16, 2)
                carry, carryb = sb(st, "carry", [N_FOX, 2], F32)
                sch.op("dve", lambda e: e.memset(onesf[:, :], 1.0), writes=[onesfb])
                sch.op("dve", lambda e: e.memset(ones3[:, :, :], 1.0), writes=[ones3b])
                sch.op("dve", lambda e: e.memset(carry[:, :], 0.0), writes=[carryb])
                for ci in range(S // CH):
                    fl, flb = fls[ci % 2]
                    cp, cpb = cpos[ci % 2]
                    cn, cnb = cneg[ci % 2]
                    sl = slice(ci * CH, (ci + 1) * CH)
                    sch.dma("sp", out=fl[:, :], in_=fl_s[:, sl], writes=[flb])
                    sch.op("dve", lambda e: e.tensor_tensor_scan(out=G[:, :], data0=onesf[:, :], data1=fl[:, :], initial=carry[:, 0:1],
                                                                 op0=ALU.mult, op1=ALU.add), reads=[onesfb, flb, carryb], writes=[Gb])
                    sch.op("dve", lambda e: e.tensor_copy(out=carry[:, 0:1], in_=G[:, CH - 1:CH]), reads=[Gb], writes=[carryb])
                    sch.op("dve", lambda e: e.tensor_scalar(out=G8[:, :], in0=G[:, :], scalar1=8.0, scalar2=None, op0=ALU.mult),
                           reads=[Gb], writes=[G8b])
                    sch.op("dve", lambda e: e.tensor_copy(out=cp[:, 0, :], in_=G8[:, :]), reads=[G8b], writes=[cpb])
                    sch.op("dve", lambda e: e.tensor_tensor(out=r1[:, :], in0=G8[:, :], in1=cp[:, 0, :], op=ALU.subtract),
                           reads=[G8b, cpb], writes=[r1b])
                    sch.op("dve", lambda e: e.tensor_copy(out=cp[:, 1, :], in_=r1[:, :]), reads=[r1b], writes=[cpb])
                    sch.op("dve", lambda e: e.tensor_tensor(out=G8[:, :], in0=r1[:, :], in1=cp[:, 1, :], op=ALU.subtract),
                           reads=[r1b, cpb], writes=[G8b])
                    sch.op("dve", lambda e: e.tensor_copy(out=cp[:, 2, :], in_=G8[:, :]), reads=[G8b], writes=[cpb])
                    sch.op("dve", lambda e: e.tensor_scalar(out=cn[:, :, :], in0=cp[:, :, :], scalar1=-1.0, scalar2=None, op0=ALU.mult),
                           reads=[cpb], writes=[cnb])
                    sch.dma("pool", out=fak_s[:, 0:3, sl], in_=cp[:, :, :], reads=[cpb])
                    sch.dma("pool", out=fak_s[:, 3:6, sl], in_=ones3[:, :, :], reads=[ones3b])
                    sch.dma("pool", out=faq_s[:, 0:3, sl], in_=ones3[:, :, :], reads=[ones3b])
                    sch.dma("pool", out=faq_s[:, 3:6, sl], in_=cn[:, :, :], reads=[cnb])
                sch.barrier()
                sch.release(keep=(cbb, prmb))

            with contextlib.ExitStack() as st:
                qTs = sbn(st, "qT", [70, S], BF16, 2)
                kTs = sbn(st, "kT", [70, S], BF16, 2)
                vs = sbn(st, "v", [128, NB, 65], BF16, 2)
                mts = sbn(st, "mt", [128, 23 * 128], F32, 2)
                Es = sbn(st, "E", [128, T], F32, 3)
                Lps = sbn(st, "Lp", [128, T], BF16, 3)
                EGs = sbn(st, "EG", [128, T], F32, 2)
                pTs = sbn(st, "pT", [128, T], BF16, 4)
                sqs = sbn(st, "sqo", [65, T], BF16, 2)
                lns = sbn(st, "lno", [64, T], F32, 2)
                rrs = sbn(st, "rro", [64, T], F32, 2)
                osb = sbn(st, "osb", [64, T], BF16, 2)
                wss = cbt[:, CB_WS:CB_WS + 64]
                for s_ in range(2):
                    vt, vb = vs[s_]
                    sch.op("dve", lambda e: e.memset(vt[:, :, 64:65], 1.0), writes=[vb])
                    sch.op("dve", lambda e: e.memset(qTs[s_][0][64:70, :], 0.0), writes=[qTs[s_][1]])
                    sch.op("dve", lambda e: e.memset(kTs[s_][0][64:70, :], 0.0), writes=[kTs[s_][1]])

                def load_head(h):
                    s_ = h % 2
                    qt, qb = qTs[s_]
                    kt, kb = kTs[s_]
                    vt, vb = vs[s_]
                    r0 = (h % 2) * 64
                    sch.dma("sp", out=qt[0:64, :], in_=qT_s[h // 2, r0:r0 + 64, :], writes=[qb])
                    sch.dma("sp", out=kt[0:64, :], in_=kT_s[h // 2, r0:r0 + 64, :], writes=[kb])
                    if h >= N_DIL + N_FOX:
                        sch.op("dve", lambda e: e.memset(kt[64:70, :], 0.0), writes=[kb])
                    if N_DIL <= h < N_DIL + N_FOX:
                        hf = h - N_DIL
                        sch.dma("sp", out=qt[64:70, :], in_=faq_s[hf], writes=[qb])
                        sch.dma("sp", out=kt[64:70, :], in_=fak_s[hf], writes=[kb])
                    sch.dma("sp", out=vt[:, :, 0:64], in_=v_s.rearrange("(n p) c -> p n c", p=128)[:, :, h * 64:(h + 1) * 64],
                            writes=[vb])
                    if h < N_DIL:
                        mt, mb = mts[s_]
                        sch.dma("sp", out=mt[:, :], in_=mt_d[h], writes=[mb])

                epi_n = [0]

                def attn_head(h):
                    s_ = h % 2
                    kind = "dil" if h < N_DIL else ("fox" if h < N_DIL + N_FOX else "sb")
                    K = 70
                    qt, qb = qTs[s_]
                    kt, kb = kTs[s_]
                    vt, vb = vs[s_]
                    mt, mb = mts[s_]
                    pairs = []
                    for i in range(NT):
                        if kind == "dil":
                            ms = [m for m in range(4 * i - 16, 4 * i + 4) if m >= 0]
                        elif kind == "fox":
                            ms = list(range(0, 4 * i + 4))
                        else:
                            ms = list(range(4 * i + 3, -1, -1))
                        for idx, m in enumerate(ms):
                            pairs.append((i, m, idx == 0, idx == len(ms) - 1))
                    n = len(pairs)

                    def stage1(pi):
                        i, m, first, last = pairs[pi]
                        stt, stb = ps[pi % 3]
                        diag = (m >= 4 * i) and kind != "dil"
                        sch.op("pe", lambda e: e.matmul(stt[:, :], lhsT=kt[0:K, m * 128:(m + 1) * 128], rhs=qt[0:K, i * T:(i + 1) * T],
                                                        start=True, stop=not diag), reads=[kb, qb], writes=[stb])
                        if diag:
                            r = m - 4 * i
                            tab = negI if kind == "fox" else negS
                            sch.op("pe", lambda e: e.matmul(stt[:, :], lhsT=ident, rhs=tab[:, 384 - 128 * r: 896 - 128 * r],
                                                            start=False, stop=True), reads=[cbb], writes=[stb])

                    def epilogue(i):
                        ot, otb = ps[5 + i % 2]
                        mst, msb = ps[7]
                        en = epi_n[0]
                        epi_n[0] += 1
                        sqt, sqb_ = sqs[en % 2]
                        lnt, lnb = lns[en % 2]
                        rrt, rrb = rrs[en % 2]
                        ost, osb_ = osb[en % 2]
                        R = 65
                        wcol = CB_WS2 if kind == "sb" else CB_WS
                        bval = EPS if kind == "sb" else 0.0
                        sch.op("act", lambda e: e.activation(out=sqt[0:R, :], in_=ot[0:R, :], func=AF.Square), reads=[otb], writes=[sqb_])
                        sch.op("pe", lambda e: e.matmul(mst[:, :], lhsT=cbt[0:R, wcol:wcol + 128], rhs=sqt[0:R, :], start=True, stop=True),
                               reads=[sqb_, cbb], writes=[msb])
                        sch.op("act", lambda e: e.activation(out=lnt[:, :], in_=mst[0:64, :], func=AF.Ln, bias=bval),
                               reads=[msb], writes=[lnb])
                        sch.op("act", lambda e: e.activation(out=rrt[:, :], in_=lnt[:, :], func=AF.Exp, scale=-0.5), reads=[lnb], writes=[rrb])
                        sch.op("dve", lambda e: e.scalar_tensor_tensor(out=ost[:, :], in0=ot[0:64, :], scalar=prm[0:64, 96 * l + 80 + h: 96 * l + 81 + h],
                                                                       in1=rrt[:, :], op0=ALU.mult, op1=ALU.mult),
                               reads=[otb, rrb, prmb], writes=[osb_])
                        r0 = (h % 2) * 64
                        sch.dma("pool", out=oT_s[h // 2, r0:r0 + 64, i * T:(i + 1) * T], in_=ost[:, :], reads=[osb_])

                    def s2_act(pi):
                        i, m, first, last = pairs[pi]
                        stt, stb = ps[pi % 3]
                        pt, pb = pTs[pi % 4]
                        if kind == "fox":
                            sch.op("act", lambda e: e.activation(out=pt[:, :], in_=stt[:, :], func=AF.Exp, scale=0.125), reads=[stb], writes=[pb])
                        elif kind == "dil":
                            Et, Eb = Es[pi % 3]
                            d0 = 4 * i - m
                            sch.op("act", lambda e: e.activation(out=Et[:, :], in_=stt[:, :], func=AF.Exp, scale=0.125), reads=[stb], writes=[Eb])
                            sch.op("dve", lambda e: e.tensor_tensor(out=pt[:, :], in0=Et[:, :], in1=mt[:, (d0 + 3) * 128:(d0 + 7) * 128], op=ALU.mult),
                                   reads=[Eb, mb], writes=[pb])
                        else:
                            Et, Eb = Es[pi % 3]
                            Lt, Lb = Lps[pi % 3]
                            sch.op("act", lambda e: e.activation(out=Et[:, :], in_=stt[:, :], func=AF.Exp, scale=0.125), reads=[stb], writes=[Eb])
                            sch.op("act", lambda e: e.activation(out=Lt[:, :], in_=Et[:, :], func=AF.Ln, bias=1.0), reads=[Eb], writes=[Lb])

                    def s3_tri1(pi):
                        i, m, first, last = pairs[pi]
                        Lt, Lb = Lps[pi % 3]
                        act_, accb = ps[3 + i % 2]
                        sch.op("pe", lambda e: e.matmul(act_[:, :], lhsT=ntri_i, rhs=Lt[:, :], start=first, stop=True, skip_group_check=not first),
                               reads=[Lb, cbb], writes=[accb])

                    def s3_eg(pi):
                        i, m, first, last = pairs[pi]
                        pt, pb = pTs[pi % 4]
                        Et, Eb = Es[pi % 3]
                        Gt, Gb_ = EGs[pi % 2]
                        act_, accb = ps[3 + i % 2]
                        sch.op("act", lambda e: e.activation(out=Gt[:, :], in_=act_[:, :], func=AF.Exp), reads=[accb], writes=[Gb_])
                        sch.op("dve", lambda e: e.tensor_tensor(out=pt[:, :], in0=Et[:, :], in1=Gt[:, :], op=ALU.mult),
                               reads=[Eb, Gb_], writes=[pb])

                    def s3_tri2(pi):
                        i, m, first, last = pairs[pi]
                        if last:
                            return
                        Lt, Lb = Lps[pi % 3]
                        act_, accb = ps[3 + i % 2]
                        sch.op("pe", lambda e: e.matmul(act_[:, :], lhsT=ntri_s, rhs=Lt[:, :], start=False, stop=True, skip_group_check=True),
                               reads=[Lb, cbb], writes=[accb])

                    def s4_pv(pi):
                        i, m, first, last = pairs[pi]
                        ot, otb = ps[5 + i % 2]
                        pt, pb = pTs[pi % 4]
                        R = 65
                        sch.op("pe", lambda e: e.matmul(ot[0:R, :], lhsT=vt[:, m, 0:R], rhs=pt[:, :], start=first, stop=last),
                               reads=[vb, pb], writes=[otb])
                        if last:
                            epilogue(i)

                    if kind == "sb":
                        stages = [(s3_tri2, 3), (s3_tri1, 2), (stage1, 0), (s2_act, 1), (s3_eg, 2), (s4_pv, 3)]
                    else:
                        stages = [(stage1, 0), (s2_act, 2), (s4_pv, 3)]
                    depth = max(lag for _, lag in stages)
                    for idx in range(n + depth):
                        for fn, lag in stages:
                            p = idx - lag
                            if 0 <= p < n:
                                fn(p)

                load_head(0)
                for h in range(NH):
                    if h + 1 < NH:
                        load_head(h + 1)
                    attn_head(h)
                sch.barrier()
                sch.release(keep=(cbb, prmb))

            with contextlib.ExitStack() as st:
                wo, wob = sb(st, "wo", [128, KC, D], BF16)
                xts = sbn(st, "xt", [128, KC, T], F32, 3)
                oTs = sbn(st, "oTt", [128, KC, T], BF16, 3)
                sq, sqb = sb(st, "sq", [128, KC, T], BF16)
                tmps = sbn(st, "tmp", [128, T], F32, 2)
                lnv, lnvb = sb(st, "lnv", [128, T], F32)
                rstd, rstdb = sb(st, "rstd", [128, T], F32)
                hTs = sbn(st, "hT", [128, KC, T], BF16, 2)
                wobs = load_w_cast(wo, w_out_d[l], "wo", D)

                def loadc1(t):
                    xt, xtb = xts[t % 3]
                    ott, otb = oTs[t % 3]
                    sch.dma("sp", out=xt[:, :, :], in_=xview(xsrc)[:, :, t * T:(t + 1) * T], writes=[xtb])
                    sch.dma("sp", out=ott[:, :, :], in_=oT_s.rearrange("k p s -> p k s")[:, :, t * T:(t + 1) * T], writes=[otb])

                def normC(t):
                    xt, xtb = xts[t % 3]
                    hT, hTb = hTs[t % 2]
                    for _ in norm_gen(l, xt, xtb, hT, hTb, ps[4 + t % 2][0], ps[4 + t % 2][1],
                                      sq, sqb, lnv, lnvb, rstd, rstdb, tmps, 24, 56):
                        yield
                    sch.dma("pool", out=xview(xT_s)[:, :, t * T:(t + 1) * T], in_=xt[:, :, :], reads=[xtb])
                    sch.dma("pool", out=h2T_s.rearrange("k p s -> p k s")[:, :, t * T:(t + 1) * T], in_=hT[:, :, :], reads=[hTb])

                loadc1(0)
                pend = iter(())
                for t in range(NT):
                    if t + 1 < NT:
                        loadc1(t + 1)
                    xt, xtb = xts[t % 3]
                    ott, otb = oTs[t % 3]
                    for c in range(KC):
                        pp, ppb = ps[c % 4]
                        for k in range(KC):
                            sch.op("pe", lambda e: e.matmul(pp[:, :], lhsT=wo[:, k, c * 128:(c + 1) * 128], rhs=ott[:, k, :],
                                                            start=(k == 0), stop=(k == KC - 1)), reads=[wobs[c // 4], otb], writes=[ppb])
                        sch.op("dve", lambda e: e.scalar_tensor_tensor(out=xt[:, c, :], in0=pp[:, :], scalar=pc(l, 16 + c), in1=xt[:, c, :],
                                                                       op0=ALU.mult, op1=ALU.add), reads=[ppb, xtb, prmb], writes=[xtb])
                        next(pend, None)
                        next(pend, None)
                    for _ in pend:
                        pass
                    pend = normC(t)
                for _ in pend:
                    pass
                sch.barrier()
                sch.release(keep=(cbb, prmb))

            for half in range(2):
                last_phase = (l == L - 1 and half == 1)
                with contextlib.ExitStack() as st:
                    HH = DFF // 2
                    w1t, w1b = sb(st, "w1t", [128, KC, HH], BF16)
                    w2t, w2b = sb(st, "w2t", [128, HH // 128, D], BF16)
                    xts = sbn(st, "xt", [128, KC, T], F32, 2)
                    hTs = sbn(st, "hT", [128, KC, T], BF16, 2)
                    rbs = sbn(st, "rb", [128, T], F32, 2)
                    hid, hidb = sb(st, "hid", [128, HH // 128, T], BF16)
                    if last_phase:
                        sq, sqb = sb(st, "sq", [128, KC, T], BF16)
                        lnv, lnvb = sb(st, "lnv", [128, T], F32)
                        rstd, rstdb = sb(st, "rstd", [128, T], F32)
                        outs = sbn(st, "ot", [128, KC, T], F32, 1)
                    w1bs = load_w_cast(w1t, w1_d[l][:, half * HH:(half + 1) * HH], "w1", HH)
                    w2bs = load_w_cast(w2t, w2_d[l][half * HH:(half + 1) * HH, :], "w2", D, nk=HH // 128, bw=256)

                    def loadc2(t):
                        xt, xtb = xts[t % 2]
                        hT, hTb = hTs[t % 2]
                        sch.dma("sp", out=xt[:, :, :], in_=xview(xT_s)[:, :, t * T:(t + 1) * T], writes=[xtb])
                        sch.dma("sp", out=hT[:, :, :], in_=h2T_s.rearrange("k p s -> p k s")[:, :, t * T:(t + 1) * T], writes=[hTb])

                    loadc2(0)
                    for t in range(NT):
                        if t + 1 < NT:
                            loadc2(t + 1)
                        xt, xtb = xts[t % 2]
                        hT, hTb = hTs[t % 2]
                        for j in range(HH // 128):
                            pp, ppb = ps[j % 4]
                            rb, rbb = rbs[j % 2]
                            for k in range(KC):
                                sch.op("pe", lambda e: e.matmul(pp[:, :], lhsT=w1t[:, k, j * 128:(j + 1) * 128], rhs=hT[:, k, :],
                                                                start=(k == 0), stop=(k == KC - 1)), reads=[w1bs[j // 4], hTb], writes=[ppb])
                            sch.op("act", lambda e: e.activation(out=rb[:, :], in_=pp[:, :], func=AF.Relu), reads=[ppb], writes=[rbb])
                            sch.op("dve", lambda e: e.scalar_tensor_tensor(out=hid[:, j, :], in0=pp[:, :], scalar=0.0, in1=rb[:, :],
                                                                           op0=ALU.max, op1=ALU.mult), reads=[ppb, rbb], writes=[hidb])
                        for c in range(KC):
                            pp, ppb = ps[4 + c % 4]
                            nj = HH // 128
                            for j in range(nj):
                                sch.op("pe", lambda e: e.matmul(pp[:, :], lhsT=w2t[:, j, c * 128:(c + 1) * 128], rhs=hid[:, j, :],
                                                                start=(j == 0), stop=(j == nj - 1)), reads=[w2bs[c // 2], hidb], writes=[ppb])
                            sch.op("dve", lambda e: e.scalar_tensor_tensor(out=xt[:, c, :], in0=pp[:, :], scalar=pc(l, 40 + c), in1=xt[:, c, :],
                                                                           op0=ALU.mult, op1=ALU.add), reads=[ppb, xtb, prmb], writes=[xtb])
                        if not last_phase:
                            sch.dma("pool", out=xview(xT_s)[:, :, t * T:(t + 1) * T], in_=xt[:, :, :], reads=[xtb])
                        else:
                            ot, otb = outs[0]
                            ssp, sspb = ps[0]
                            rms_rstd(xt, xtb, sq, sqb, ssp, sspb, lnv, lnvb, rstd, rstdb)
                            for k in range(KC):
                                sch.op("dve", lambda e: e.scalar_tensor_tensor(out=ot[:, k, :], in0=xt[:, k, :], scalar=prm[:, 192 + k:193 + k],
                                                                               in1=rstd[:, :], op0=ALU.mult, op1=ALU.mult),
                                       reads=[xtb, rstdb, prmb], writes=[otb])
                            sch.dma("pool", out=xview(outT)[:, :, t * T:(t + 1) * T], in_=ot[:, :, :], reads=[otb])
                    sch.barrier()
                    sch.release(keep=(cbb, prmb))
                sch.release(keep=(cbb, prmb))
        sch.barrier()
        print("instructions emitted:", sch.ninst)
    return nc


_CACHE = {}


def kernel(x, c, w_mod, b_mod, g_norm1, w_in, b_f, g_out, w_out, g_norm2, w_mlp_in, w_mlp_out, g_final):
    x = np.asarray(x, np.float32)
    B, S, _ = x.shape
    L = int(np.asarray(w_mod).shape[0])
    assert B == N_CORES
    key = (S, L)
    if key not in _CACHE:
        _CACHE[key] = build(S, L)
    nc = _CACHE[key]
    mt, cb = _const_tables()
    f = lambda a: np.ascontiguousarray(np.asarray(a, np.float32))
    col = lambda a: np.ascontiguousarray(np.asarray(a, np.float32).reshape(-1, 128).T)
    shared = {
        "w_mod": f(w_mod),
        "bmod_col": np.stack([col(np.asarray(b_mod)[l]) for l in range(L)]),
        "g1_col": np.stack([col(np.asarray(g_norm1)[l]) for l in range(L)]),
        "g2_col": np.stack([col(np.asarray(g_norm2)[l]) for l in range(L)]),
        "gout_col": np.stack([np.ascontiguousarray(np.asarray(g_out, np.float32)[l].reshape(NH, HD).T) for l in range(L)]),
        "gf_col": col(g_final),
        "bf_col": f(b_f).reshape(L, N_FOX, 1),
        "w_in": f(w_in), "w_out": f(w_out), "w1": f(w_mlp_in), "w2": f(w_mlp_out),
        "mtab": mt, "cb": cb,
    }
    c = np.asarray(c, np.float32)
    in_maps = []
    for b in range(B):
        m = dict(shared)
        m["xT"] = np.ascontiguousarray(x[b].T)
        m["c_col"] = col(c[b])
        in_maps.append(m)
    res = run_bass_kernel_spmd(nc, in_maps, core_ids=list(range(N_CORES)))
    out = np.stack([np.ascontiguousarray(res.results[b]["outT"].T) for b in range(B)], axis=0)
    return out.astype(np.float32)
```
